# Optimizing a Trainium2 kernel written in Bass

```python
import math
import jax, jax.numpy as jnp
from jax import lax
import numpy as np

D_MODEL = 1024
BATCH = 32
SEQ = 2048
DEPTH = 4

CHUNK = 64
RET_HEADS = 4
RET_DK = 128
RET_DV = 256
CA_HEADS = 8
CA_DH = 64
CA_LEFT_CHUNKS = 8
REL_CLIP = 256
DA_HEADS = 4
DA_DH = 64
Q_BLOCK = 128
FFN_HIDDEN = -(-8 * D_MODEL // (3 * 256)) * 256
ROPE_THETA = 10000.0
LN_EPS = 1e-5
NEG_INF = -1e30
DEEPNORM_ALPHA = (2.0 * DEPTH) ** 0.25
DEEPNORM_BETA = (8.0 * DEPTH) ** -0.25
N_BRANCHES = 3

RET_QK_W = RET_HEADS * RET_DK
RET_V_W = RET_HEADS * RET_DV
CA_W = CA_HEADS * CA_DH
DA_QK_W = 2 * DA_HEADS * DA_DH
DA_V_W = DA_HEADS * 2 * DA_DH
IN_WIDTHS = (RET_QK_W, RET_QK_W, RET_V_W, RET_V_W, CA_W, CA_W, CA_W, DA_QK_W, DA_QK_W, DA_V_W, N_BRANCHES * D_MODEL)
IN_IS_VALUE = (False, False, True, False, False, False, True, False, False, True, False)
IN_TOTAL = sum(IN_WIDTHS)

kernel_name = "hybrid_retention_chunkattn_diffattn_deepnorm"


def _layer_norm(x, g, b):
    xf = x.astype(jnp.float32)
    mu = jnp.mean(xf, axis=-1, keepdims=True)
    var = jnp.mean(jnp.square(xf - mu), axis=-1, keepdims=True)
    return ((xf - mu) * lax.rsqrt(var + LN_EPS) * g.astype(jnp.float32) + b.astype(jnp.float32)).astype(x.dtype)


def _rms_norm(x, g):
    xf = x.astype(jnp.float32)
    y = xf * lax.rsqrt(jnp.mean(xf * xf, axis=-1, keepdims=True) + LN_EPS) * g.astype(jnp.float32)
    return y.astype(x.dtype)


def _rotary(x):
    S, d = x.shape[1], x.shape[-1]
    inv_freq = ROPE_THETA ** (-jnp.arange(0, d, 2, dtype=jnp.float32) / d)
    ang = jnp.arange(S, dtype=jnp.float32)[:, None] * inv_freq[None, :]
    cos = jnp.cos(ang)[None, :, None, :].astype(x.dtype)
    sin = jnp.sin(ang)[None, :, None, :].astype(x.dtype)
    x1, x2 = x[..., : d // 2], x[..., d // 2:]
    return jnp.concatenate([x1 * cos - x2 * sin, x2 * cos + x1 * sin], axis=-1)


def _retention(q, k, v, gate, norm_g):
    B, S, H, dk = q.shape
    dv = v.shape[-1]
    nc = S // CHUNK
    q = _rotary(q)
    k = _rotary(k) * (dk ** -0.5)
    log_gamma = jnp.log1p(-jnp.exp2(-5.0 - jnp.arange(H, dtype=jnp.float32)))
    pos = jnp.arange(CHUNK, dtype=jnp.float32)
    intra_decay = jnp.exp(log_gamma[:, None, None] * jnp.abs(pos[:, None] - pos[None, :]))
    q_decay = jnp.exp(log_gamma[None, :] * (pos[:, None] + 1.0))
    k_decay = jnp.exp(log_gamma[None, :] * (CHUNK - 1.0 - pos[:, None]))
    chunk_decay = jnp.exp(log_gamma * CHUNK)

    qc = q.reshape(B, nc, CHUNK, H, dk)
    kc = k.reshape(B, nc, CHUNK, H, dk)
    vc = v.reshape(B, nc, CHUNK, H, dv)
    scores = jnp.einsum('bcnhd,bcmhd->bchnm', qc, kc) * intra_decay.astype(q.dtype)
    o_intra = jnp.einsum('bchnm,bcmhe->bcnhe', scores, vc)

    def step(state, xs):
        q_i, k_i, v_i = xs
        o = jnp.einsum('bnhd,bhde->bnhe', q_i.astype(jnp.float32) * q_decay[None, :, :, None], state)
        kv = jnp.einsum('bmhd,bmhe->bhde', k_i.astype(jnp.float32) * k_decay[None, :, :, None], v_i.astype(jnp.float32))
        state = state * chunk_decay[None, :, None, None] + kv
        return state, o

    state0 = jnp.zeros((B, H, dk, dv), jnp.float32)
    _, o_inter = lax.scan(step, state0, (jnp.swapaxes(qc, 0, 1), jnp.swapaxes(kc, 0, 1), jnp.swapaxes(vc, 0, 1)))
    o = (o_intra.astype(jnp.float32) + jnp.swapaxes(o_inter, 0, 1)).astype(v.dtype)
    o = _rms_norm(o, norm_g).reshape(B, S, H * dv)
    return jax.nn.silu(gate) * o


def _chunk_attention(q, k, v, rel_bias):
    B, S, H, d = q.shape
    nc = S // CHUNK
    pad = CA_LEFT_CHUNKS * CHUNK
    band = pad + CHUNK
    k_pad = jnp.pad(k, ((0, 0), (pad, 0), (0, 0), (0, 0)))
    v_pad = jnp.pad(v, ((0, 0), (pad, 0), (0, 0), (0, 0)))
    qpos = jnp.arange(CHUNK)
    kpos = jnp.arange(band)
    rel = qpos[:, None] + pad - kpos[None, :]
    idx = jnp.clip(rel, -REL_CLIP, REL_CLIP) + REL_CLIP
    bias = rel_bias[:, idx].astype(jnp.float32)
    q = q * (d ** -0.5)

    def one_chunk(i):
        start = i * CHUNK
        q_i = lax.dynamic_slice_in_dim(q, start, CHUNK, axis=1)
        k_i = lax.dynamic_slice_in_dim(k_pad, start, band, axis=1)
        v_i = lax.dynamic_slice_in_dim(v_pad, start, band, axis=1)
        s = jnp.einsum('bqhd,bkhd->bhqk', q_i, k_i).astype(jnp.float32) + bias[None]
        valid = (start - pad + kpos) >= 0
        s = jnp.where(valid[None, None, None, :], s, NEG_INF)
        p = jax.nn.softmax(s, axis=-1).astype(v.dtype)
        return jnp.einsum('bhqk,bkhd->bqhd', p, v_i)

    o = lax.map(one_chunk, jnp.arange(nc))
    return jnp.transpose(o, (1, 0, 2, 3, 4)).reshape(B, S, H * d)


def _diff_attention(q, k, v, lam, lambda_init, norm_g):
    B, S, H2, d = q.shape
    H = H2 // 2
    q = _rotary(q) * (d ** -0.5)
    k = _rotary(k)
    nb = S // Q_BLOCK
    key_chunk = jnp.arange(S) // CHUNK

    def one_block(i):
        start = i * Q_BLOCK
        q_i = lax.dynamic_slice_in_dim(q, start, Q_BLOCK, axis=1)
        s = jnp.einsum('bqgd,bkgd->bgqk', q_i, k).astype(jnp.float32)
        query_chunk = (start + jnp.arange(Q_BLOCK)) // CHUNK
        allowed = key_chunk[None, :] <= query_chunk[:, None]
        s = jnp.where(allowed[None, None], s, NEG_INF)
        p = jax.nn.softmax(s, axis=-1).reshape(B, H, 2, Q_BLOCK, S)
        p = p[:, :, 0] - lam * p[:, :, 1]
        return jnp.einsum('bhqk,bkhe->bqhe', p.astype(v.dtype), v)

    o = lax.map(one_block, jnp.arange(nb))
    o = jnp.transpose(o, (1, 0, 2, 3, 4)).reshape(B, S, H, 2 * d)
    o = _rms_norm(o, norm_g) * (1.0 - lambda_init)
    return o.reshape(B, S, H * 2 * d)


def setup_inputs(seed: int = 0) -> dict:
    key = jax.random.key(seed)
    ks = jax.random.split(key, 24)

    def nrm(k, shape, scale):
        return jax.random.normal(k, shape, jnp.float32) * scale

    beta = DEEPNORM_BETA
    col_scale = jnp.concatenate([
        jnp.full((w,), beta if is_v else 1.0, jnp.float32) for w, is_v in zip(IN_WIDTHS, IN_IS_VALUE)])
    x = nrm(ks[0], (BATCH, SEQ, D_MODEL), 1.0)
    w_in = nrm(ks[1], (DEPTH, D_MODEL, IN_TOTAL), D_MODEL ** -0.5) * col_scale
    ret_norm_g = 1.0 + nrm(ks[2], (DEPTH, RET_DV), 0.02)
    ca_rel_bias = nrm(ks[3], (DEPTH, CA_HEADS, 2 * REL_CLIP + 1), 0.1)
    da_lambda_q1 = nrm(ks[4], (DEPTH, DA_DH), 0.1)
    da_lambda_k1 = nrm(ks[5], (DEPTH, DA_DH), 0.1)
    da_lambda_q2 = nrm(ks[6], (DEPTH, DA_DH), 0.1)
    da_lambda_k2 = nrm(ks[7], (DEPTH, DA_DH), 0.1)
    da_norm_g = 1.0 + nrm(ks[8], (DEPTH, 2 * DA_DH), 0.02)
    w_branch_a = nrm(ks[9], (DEPTH, RET_V_W, D_MODEL), RET_V_W ** -0.5 * beta)
    w_branch_b = nrm(ks[10], (DEPTH, CA_W, D_MODEL), CA_W ** -0.5 * beta)
    w_branch_c = nrm(ks[11], (DEPTH, DA_V_W, D_MODEL), DA_V_W ** -0.5 * beta)
    b_merge = nrm(ks[12], (DEPTH, N_BRANCHES * D_MODEL), 0.1)
    w_out = nrm(ks[13], (DEPTH, D_MODEL, D_MODEL), D_MODEL ** -0.5 * beta)
    ln1_g = 1.0 + nrm(ks[14], (DEPTH, D_MODEL), 0.02)
    ln1_b = nrm(ks[15], (DEPTH, D_MODEL), 0.02)
    w_ffn_in = nrm(ks[16], (DEPTH, D_MODEL, 2 * FFN_HIDDEN), D_MODEL ** -0.5 * beta)
    w_ffn_out = nrm(ks[17], (DEPTH, FFN_HIDDEN, D_MODEL), FFN_HIDDEN ** -0.5 * beta)
    ln2_g = 1.0 + nrm(ks[18], (DEPTH, D_MODEL), 0.02)
    ln2_b = nrm(ks[19], (DEPTH, D_MODEL), 0.02)
    return {
        "x": x, "w_in": w_in, "ret_norm_g": ret_norm_g, "ca_rel_bias": ca_rel_bias,
        "da_lambda_q1": da_lambda_q1, "da_lambda_k1": da_lambda_k1,
        "da_lambda_q2": da_lambda_q2, "da_lambda_k2": da_lambda_k2, "da_norm_g": da_norm_g,
        "w_branch_a": w_branch_a, "w_branch_b": w_branch_b, "w_branch_c": w_branch_c,
        "b_merge": b_merge, "w_out": w_out, "ln1_g": ln1_g, "ln1_b": ln1_b,
        "w_ffn_in": w_ffn_in, "w_ffn_out": w_ffn_out, "ln2_g": ln2_g, "ln2_b": ln2_b,
    }


def reference(x, w_in, ret_norm_g, ca_rel_bias, da_lambda_q1, da_lambda_k1, da_lambda_q2, da_lambda_k2,
              da_norm_g, w_branch_a, w_branch_b, w_branch_c, b_merge, w_out, ln1_g, ln1_b,
              w_ffn_in, w_ffn_out, ln2_g, ln2_b):
    B, S, D = x.shape
    splits = [int(s) for s in np.cumsum(IN_WIDTHS)[:-1]]
    for l in range(DEPTH):
        h = jnp.einsum('bsd,de->bse', x, w_in[l])
        rq, rk, rv, rg, cq, ck, cv, dq, dk, dv, gates = jnp.split(h, splits, axis=-1)

        o_a = _retention(rq.reshape(B, S, RET_HEADS, RET_DK), rk.reshape(B, S, RET_HEADS, RET_DK),
                         rv.reshape(B, S, RET_HEADS, RET_DV), rg, ret_norm_g[l])
        o_b = _chunk_attention(cq.reshape(B, S, CA_HEADS, CA_DH), ck.reshape(B, S, CA_HEADS, CA_DH),
                               cv.reshape(B, S, CA_HEADS, CA_DH), ca_rel_bias[l])
        lambda_init = 0.8 - 0.6 * math.exp(-0.3 * l)
        lam = (jnp.exp(jnp.sum(da_lambda_q1[l].astype(jnp.float32) * da_lambda_k1[l].astype(jnp.float32)))
               - jnp.exp(jnp.sum(da_lambda_q2[l].astype(jnp.float32) * da_lambda_k2[l].astype(jnp.float32)))
               + lambda_init)
        o_c = _diff_attention(dq.reshape(B, S, 2 * DA_HEADS, DA_DH), dk.reshape(B, S, 2 * DA_HEADS, DA_DH),
                              dv.reshape(B, S, DA_HEADS, 2 * DA_DH), lam, lambda_init, da_norm_g[l])

        g = jax.nn.sigmoid(gates + b_merge[l]).reshape(B, S, N_BRANCHES, D)
        merged = (g[:, :, 0] * jnp.einsum('bse,ed->bsd', o_a, w_branch_a[l])
                  + g[:, :, 1] * jnp.einsum('bse,ed->bsd', o_b, w_branch_b[l])
                  + g[:, :, 2] * jnp.einsum('bse,ed->bsd', o_c, w_branch_c[l]))
        mix = jnp.einsum('bsd,de->bse', merged, w_out[l])
        x = _layer_norm(DEEPNORM_ALPHA * x + mix, ln1_g[l], ln1_b[l])

        u = jnp.einsum('bsd,df->bsf', x, w_ffn_in[l])
        u_gate, u_up = jnp.split(u, 2, axis=-1)
        ffn = jnp.einsum('bsf,fd->bsd', jax.nn.silu(u_gate) * u_up, w_ffn_out[l])
        x = _layer_norm(DEEPNORM_ALPHA * x + ffn, ln2_g[l], ln2_b[l])
    return x
```

```python
import math
from contextlib import ExitStack
import numpy as np
import concourse.bass as bass
import concourse.mybir as mybir
from concourse.bass_utils import run_bass_kernel_spmd

F32 = mybir.dt.float32
BF16 = mybir.dt.bfloat16
ALU = mybir.AluOpType
AF = mybir.ActivationFunctionType
AX = mybir.AxisListType

S = 2048
D = 1024
L = 4
NTT = 4
TT = 512
ALPHA = (2.0 * L) ** 0.25
EPS = 1e-5
NEG = -30000.0
WI_COLS = 11264
N_CORES = 8


class Sched:
    ENGS = ("pe", "act", "dve", "pool", "sp")
    DMA_RING = 8

    def __init__(self, nc):
        self.nc = nc
        self.ops = []
        self.last_w = {}
        self.readers = {}
        self.dma_count = {e: 0 for e in self.ENGS}
        self.eng_last = {e: None for e in self.ENGS}
        self.dma_last = {}
        self.pending_barrier = {e: None for e in self.ENGS}

    def add(self, eng, fn, reads=(), writes=(), dma=False):
        idx = len(self.ops)
        def _isps(k):
            return isinstance(k, str) and k.startswith("ps") and k[2:3].isdigit()
        psk = [k.split("_")[0] for k in list(reads) + list(writes) if _isps(k)]
        reads = [k for k in reads if not _isps(k)]
        writes = [k for k in writes if not _isps(k)] + sorted(set(psk))
        raw = set()
        oth = set()
        for r in reads:
            if r in self.last_w:
                raw.add(self.last_w[r])
        for w in writes:
            if w in self.last_w:
                oth.add(self.last_w[w])
            oth.update(self.readers.get(w, ()))
        if self.pending_barrier[eng] is not None:
            oth.update(self.pending_barrier[eng])
            self.pending_barrier[eng] = None
        for r in reads:
            self.readers.setdefault(r, []).append(idx)
        for w in writes:
            self.last_w[w] = idx
            self.readers[w] = []
        raw.discard(idx)
        oth.discard(idx)
        best = {}
        keep_raw, keep_oth = set(), set()
        for d in raw | oth:
            dop = self.ops[d]
            if dop["dma"]:
                (keep_raw if d in raw else keep_oth).add(d)
                continue
            if dop["eng"] == eng and not dma:
                if d not in raw:
                    continue
            cur = best.get(dop["eng"])
            if cur is None or d > cur:
                best[dop["eng"]] = d
        for d in best.values():
            (keep_raw if d in raw else keep_oth).add(d)
        raw, oth = keep_raw, keep_oth
        op = dict(eng=eng, fn=fn, raw=raw, oth=oth - raw, dma=dma, sig=False, val=None)
        if dma:
            n = self.dma_count[eng]
            self.dma_count[eng] += 1
            op["slot"] = n % self.DMA_RING
            op["val"] = 16 * (n // self.DMA_RING + 1)
            self.dma_last[(eng, op["slot"])] = idx
        self.ops.append(op)
        self.eng_last[eng] = idx
        return idx

    def barrier(self):
        pts = set(v for v in self.eng_last.values() if v is not None)
        pts.update(self.dma_last.values())
        for e in self.ENGS:
            cur = self.pending_barrier[e] or set()
            self.pending_barrier[e] = set(cur) | pts

    def emit(self, sems, block):
        ops = self.ops
        for op in ops:
            for d in op["raw"] | op["oth"]:
                dop = ops[d]
                if dop["dma"]:
                    continue
                if dop["eng"] != op["eng"]:
                    dop["sig"] = True
                elif op["dma"]:
                    dop["sig"] = True
                elif d in op["raw"] and op["eng"] in ("act", "dve", "pool"):
                    dop["sig"] = True
        cnt = {e: 0 for e in self.ENGS}
        for op in ops:
            if not op["dma"] and op["sig"]:
                cnt[op["eng"]] += 1
                op["val"] = cnt[op["eng"]]
        per_eng = {e: [] for e in self.ENGS}
        for i, op in enumerate(ops):
            per_eng[op["eng"]].append(i)
        final_dma = {}
        for op in ops:
            if op["dma"]:
                final_dma[("dma", op["eng"], op["slot"])] = op["val"]
        self.nwaits = 0

        def run(e, E):
            waited = {}
            for i in per_eng[e]:
                op = ops[i]
                need = {}
                for d in op["raw"] | op["oth"]:
                    dop = ops[d]
                    if dop["dma"]:
                        key = ("dma", dop["eng"], dop["slot"])
                    else:
                        if not dop["sig"]:
                            continue
                        if dop["eng"] == e and not (d in op["raw"] or op["dma"]):
                            continue
                        if dop["eng"] == e and e == "pe":
                            continue
                        key = dop["eng"]
                    need[key] = max(need.get(key, 0), dop["val"])
                if op["dma"] and op["val"] > 16:
                    key = ("dma", e, op["slot"])
                    need[key] = max(need.get(key, 0), op["val"] - 16)
                op["waits"] = []
                for key, v in need.items():
                    if waited.get(key, 0) >= v:
                        continue
                    E.wait_ge(sems[key], v)
                    waited[key] = v
                    self.nwaits += 1
                    op["waits"].append((key, v))
                ins = op["fn"](E)
                if op["dma"]:
                    ins.then_inc(sems[("dma", e, op["slot"])], 16)
                elif op["sig"]:
                    ins.then_inc(sems[e], 1)
            if e == "sp":
                for key, v in final_dma.items():
                    E.wait_ge(sems[key], v)

        block.tensor(lambda E: run("pe", E))
        block.scalar(lambda E: run("act", E))
        block.vector(lambda E: run("dve", E))
        block.gpsimd(lambda E: run("pool", E))
        block.sync(lambda E: run("sp", E))
        self.stats = dict(nops=len(ops), nwaits=self.nwaits, sig=cnt)
        self.per_eng = per_eng


class Ring:
    def __init__(self, items):
        self.items = items
        self.i = 0

    def next(self):
        it = self.items[self.i % len(self.items)]
        self.i += 1
        return it


def _rot_tables():
    t = np.arange(S, dtype=np.float32)
    out = np.zeros((4, 2, 128, S), np.float32)
    p = np.arange(128)
    inv = (10000.0 ** (-np.arange(0, 128, 2, dtype=np.float32) / 128)).astype(np.float32)
    ang = (t[:, None] * inv[None, :]).astype(np.float32)
    cos = np.cos(ang).astype(np.float32).T
    sin = np.sin(ang).astype(np.float32).T
    fi = p % 64
    sgn = np.where(p < 64, -1.0, 1.0).astype(np.float32)
    for ty, sc in ((0, 1.0), (1, 128 ** -0.5)):
        out[ty, 0] = cos[fi] * np.float32(sc)
        out[ty, 1] = sin[fi] * sgn[:, None] * np.float32(sc)
    inv = (10000.0 ** (-np.arange(0, 64, 2, dtype=np.float32) / 64)).astype(np.float32)
    ang = (t[:, None] * inv[None, :]).astype(np.float32)
    cos = np.cos(ang).astype(np.float32).T
    sin = np.sin(ang).astype(np.float32).T
    pp = p % 64
    fi = pp % 32
    sgn = np.where(pp < 32, -1.0, 1.0).astype(np.float32)
    for ty, sc in ((2, 64 ** -0.5), (3, 1.0)):
        out[ty, 0] = cos[fi] * np.float32(sc)
        out[ty, 1] = sin[fi] * sgn[:, None] * np.float32(sc)
    return out


def _ret_decay():
    out = np.zeros((4, 5, 128, 512), np.float64)
    i = np.arange(128)[:, None]
    j = np.arange(512)[None, :]
    for h in range(4):
        lg = math.log1p(-(2.0 ** (-5.0 - h)))
        out[h, 0] = np.exp(lg * (128 + j - i))
        for dd in range(4):
            m = dd * 128 + i
            kc = m // 64
            qc = j // 64
            rel = j - m
            out[h, 1 + dd] = np.where(kc == qc, np.exp(lg * np.abs(rel)), np.where(kc < qc, np.exp(lg * rel), 0.0))
    return out.astype(np.float32)


def _da_mask():
    out = np.zeros((4, 128, 512), np.float32)
    i = np.arange(128)[:, None]
    j = np.arange(512)[None, :]
    for dd in range(4):
        kc = (dd * 128 + i) // 64
        qc = j // 64
        out[dd] = np.where(kc <= qc, 0.0, NEG)
    return out


def _ca_mask():
    out = np.zeros((8, 128, 512), np.float32)
    i = np.arange(128)[:, None]
    j = np.arange(512)[None, :]
    for b in range(8):
        kc = 2 * b + i // 64 - 8
        a = j // 64
        out[b] = np.where((kc >= a - 8) & (kc <= a), 0.0, NEG)
    return out


def _ret_scalar(h, delta):
    lg = math.log1p(-(2.0 ** (-5.0 - h)))
    return float(math.exp(lg * (delta - 128)))


def build_program(n_seq, n_layers=L, debug=False):
    nc = bass.Bass("TRN2", target_bir_lowering=False)
    dt_in = lambda name, shape, dt=F32: nc.dram_tensor(name, shape, dt, kind="ExternalInput").ap()
    def scr(name, shape, dt):
        kind = "ExternalOutput" if (debug and name in ("qk_d", "v_d", "rg_d", "oT_d")) else "Internal"
        return nc.dram_tensor(name, shape, dt, kind=kind).ap()
    x_in = dt_in("x", [n_seq, S, D])
    wi_f = dt_in("wi", [L, D, WI_COLS])
    wbr_f = dt_in("wbr", [L, 2048, D])
    wo_f = dt_in("wo", [L, D, D])
    wfi_f = dt_in("wfi", [L, D, 5632])
    wfo_f = dt_in("wfo", [L, 2816, D])
    pp_in = dt_in("pp", [L, 128, 56])
    bc_in = dt_in("bc", [L, 128, 896])
    caext = dt_in("caext", [L, 8, 1536])
    rot_in = dt_in("rot", [4, 2, 128, S])
    dec_in = dt_in("dec", [4, 5, 128, 512])
    dam_in = dt_in("dam", [4, 128, 512])
    cam_in = dt_in("cam", [8, 128, 512])
    out_d = nc.dram_tensor("out", [n_seq, S, D], F32, kind="ExternalOutput").ap()

    wi_b = scr("wi_b", [L, D, WI_COLS], BF16)
    wbr_b = scr("wbr_b", [L, 2048, D], BF16)
    wo_b = scr("wo_b", [L, D, D], BF16)
    wfi_b = scr("wfi_b", [L, D, 5632], BF16)
    wfo_b = scr("wfo_b", [L, 2816, D], BF16)
    xres_d = scr("xres_d", [128, 8, S], F32)
    qk_d = scr("qk_d", [24, 128, S], BF16)
    v_d = scr("v_d", [S, 2048], BF16)
    rg_d = scr("rg_d", [S, 1024], F32)
    oT_d = scr("oT_d", [16, 128, S], BF16)
    with ExitStack() as es:
        SC = Sched(nc)
        sems = {}
        for e in ("pe", "act", "dve", "pool"):
            sems[e] = es.enter_context(nc.semaphore("s_" + e))
        for q in ("sp", "pool", "act"):
            for k in range(Sched.DMA_RING):
                sems[("dma", q, k)] = es.enter_context(nc.semaphore(f"d_{q}{k}"))
        psb = [es.enter_context(nc.psum_tensor(f"ps{i}", [128, 512], F32)) for i in range(7)]
        ps7 = es.enter_context(nc.psum_tensor("ps7", [128, 1024], BF16))

        uniq = [0]

        def sbuf(stack, name, shape, dt):
            uniq[0] += 1
            return stack.enter_context(nc.sbuf_tensor(f"{name}_u{uniq[0]}", shape, dt))

        xT = sbuf(es, "xT", [128, 8, S], BF16)
        idb = sbuf(es, "idb", [128, 128], BF16)
        idf = sbuf(es, "idf", [128, 128], F32)
        antiI = sbuf(es, "antiI", [128, 128], F32)
        onesf = sbuf(es, "onesf", [128, 128], F32)
        ppt = sbuf(es, "ppt", [128, 56], F32)
        bct = sbuf(es, "bct", [128, 896], F32)
        gda = sbuf(es, "gda", [128, 128], F32)
        nlam = sbuf(es, "nlam", [128, 1], F32)
        sm = sbuf(es, "sm", [128, 8], F32)
        lp = sbuf(es, "lp", [128, 64], F32)

        block = es.enter_context(nc.Block())

        def A(eng, fn, r=(), w=()):
            SC.add(eng, fn, reads=r, writes=w)

        def DMA(q, out_ap, in_ap, r=(), w=()):
            SC.add(q, lambda E: E.dma_start(out=out_ap, in_=in_ap), reads=r, writes=w, dma=True)

        def DMA2(q, out_ap, in_ap, r, wkey):
            a = out_ap.shape[1]
            hlf = a // 2
            keys = [(wkey, 0), (wkey, 1)]
            DMA(q, out_ap[:, 0:hlf, :], in_ap[:, 0:hlf, :], r=r, w=[keys[0], wkey])
            DMA(q, out_ap[:, hlf:a, :], in_ap[:, hlf:a, :], r=r, w=[keys[1], wkey])
            return keys

        A("pool", lambda E: E.memset(idb[:], 0.0), w=["idb"])
        A("pool", lambda E: E.affine_select(out=idb[:], in_=idb[:], compare_op=ALU.not_equal, fill=1.0, base=0,
                                            pattern=[[-1, 128]], channel_multiplier=1), r=["idb"], w=["idb"])
        A("pool", lambda E: E.memset(idf[:], 0.0), w=["idf"])
        A("pool", lambda E: E.affine_select(out=idf[:], in_=idf[:], compare_op=ALU.not_equal, fill=1.0, base=0,
                                            pattern=[[-1, 128]], channel_multiplier=1), r=["idf"], w=["idf"])
        A("pool", lambda E: E.memset(antiI[:], 0.0), w=["antiI"])
        A("pool", lambda E: E.affine_select(out=antiI[:], in_=antiI[:], compare_op=ALU.not_equal, fill=1.0, base=-127,
                                            pattern=[[1, 128]], channel_multiplier=1), r=["antiI"], w=["antiI"])
        A("pool", lambda E: E.memset(onesf[:], 1.0 / D), w=["onesf"])
        wkeys = {}
        import os as _os

        def convert_layer(l):
            for (src, dst, rows, nm) in ((wi_f, wi_b, D, "wi"), (wbr_f, wbr_b, 2048, "wbr"), (wo_f, wo_b, D, "wo"),
                                         (wfi_f, wfi_b, D, "wfi"), (wfo_f, wfo_b, 2816, "wfo")):
                ncols = src.shape[2]
                for r0 in range(0, rows, 128):
                    for c0 in range(0, ncols, 2048):
                        c1 = min(ncols, c0 + 2048)
                        DMA("pool", dst[l, r0:r0 + 128, c0:c1], src[l, r0:r0 + 128, c0:c1], w=[(nm, l, r0, c0)])

        for l in range(n_layers):
            for nm_, rows_, ncols_ in (("wi", D, WI_COLS), ("wbr", 2048, D), ("wo", D, D), ("wfi", D, 5632), ("wfo", 2816, D)):
                wkeys[(nm_, l)] = [(nm_, l, r0, c0) for r0 in range(0, rows_, 128) for c0 in range(0, ncols_, 2048)]
        convert_layer(0)

        mmring = Ring([(f"ps{i}", psb[i]) for i in range(4)])

        def load_layer_params(l):
            DMA("sp", ppt[:], pp_in[l], w=["ppt"])
            DMA("sp", bct[:], bc_in[l], w=["bct"])
            lam_init = 0.8 - 0.6 * math.exp(-0.3 * l)
            A("dve", lambda E: E.tensor_tensor(lp[:], bct[:, 640:704], bct[:, 704:768], ALU.mult), r=["bct"], w=["lp"])
            A("dve", lambda E: E.reduce_sum(sm[:, 0:1], lp[:], axis=AX.X), r=["lp"], w=["sm0"])
            A("dve", lambda E: E.tensor_tensor(lp[:], bct[:, 768:832], bct[:, 832:896], ALU.mult), r=["bct", "sm0"], w=["lp"])
            A("dve", lambda E: E.reduce_sum(sm[:, 1:2], lp[:], axis=AX.X), r=["lp"], w=["sm1"])
            A("act", lambda E: E.activation(sm[:, 2:3], sm[:, 0:1], AF.Exp), r=["sm0"], w=["sm2"])
            A("act", lambda E: E.activation(sm[:, 3:4], sm[:, 1:2], AF.Exp), r=["sm1"], w=["sm3"])
            A("dve", lambda E: E.tensor_tensor(sm[:, 4:5], sm[:, 2:3], sm[:, 3:4], ALU.subtract), r=["sm2", "sm3"], w=["sm4"])
            A("dve", lambda E: E.tensor_scalar(nlam[:], sm[:, 4:5], lam_init, -1.0, ALU.add, ALU.mult), r=["sm4"], w=["nlam"])
            A("dve", lambda E: E.tensor_scalar_mul(gda[:], bct[:, 512:640], 1.0 - lam_init), r=["bct"], w=["gda"])

        def phase_input(sq):
            with ExitStack() as st:
                xin = [sbuf(st, f"xin{i}", [128, 4, D], F32) for i in range(2)]
                xr = [sbuf(st, f"xri{i}", [128, 8, TT], F32) for i in range(2)]
                for tt in range(NTT):
                    xi = xin[tt % 2]
                    xo = xr[tt % 2]
                    ki, ko = f"xin{tt % 2}", f"xri{tt % 2}"
                    DMA("sp", xi[:], x_in[sq, tt * TT:(tt + 1) * TT, :].rearrange("(s p) d -> p s d", p=128), w=[ki])
                    for c in range(8):
                        pk, pb = mmring.next()
                        for s in range(4):
                            A("pe", (lambda pb=pb, xi=xi, s=s, c=c: lambda E: E.transpose(pb[:, s * 128:(s + 1) * 128], xi[:, s, c * 128:(c + 1) * 128], idf[:]))(),
                              r=[ki, "idf"], w=[pk])
                        A("act", (lambda pb=pb, xo=xo, c=c: lambda E: E.copy(xo[:, c, :], pb[:]))(), r=[pk], w=[ko])
                        A("dve", (lambda xo=xo, c=c, tt=tt: lambda E: E.tensor_copy(xT[:, c, tt * TT:(tt + 1) * TT], xo[:, c, :]))(), r=[ko], w=[("xT", tt)])
                    DMA("pool", xres_d[:, :, tt * TT:(tt + 1) * TT], xo[:], r=[ko], w=[("xres", tt)])

        def phase_P(l):
            with ExitStack() as st:
                wA = [sbuf(st, f"wA{i}", [128, 8, 512], BF16) for i in range(2)]
                wB = [sbuf(st, f"wB{i}", [128, 8, 512], BF16) for i in range(2)]
                rc = sbuf(st, "rc", [128, S], F32)
                rs = sbuf(st, "rs", [128, S], F32)
                t1 = [sbuf(st, f"t1_{i}", [128, 512], F32) for i in range(2)]
                t2 = [sbuf(st, f"t2_{i}", [128, 512], F32) for i in range(2)]
                qo = [sbuf(st, f"qo{i}", [128, S], BF16) for i in range(2)]
                vb = [sbuf(st, f"vb{i}", [128, 4, 512], BF16) for i in range(2)]
                sg = [sbuf(st, f"sg{i}", [128, 512], F32) for i in range(2)]
                rgb = [sbuf(st, f"rgb{i}", [128, 4, 512], F32) for i in range(2)]
                wl = wi_b[l].rearrange("(c p) f -> p c f", p=128)
                nq = 0
                for ty in range(4):
                    wa, wb_ = wA[ty % 2], wB[ty % 2]
                    ka, kb = f"wA{ty % 2}", f"wB{ty % 2}"
                    DMA("sp", wa[:], wl[:, :, ty * 512:(ty + 1) * 512], r=wkeys[("wi", l)], w=[ka])
                    DMA("sp", wb_[:], wl[:, :, 2048 + ty * 512:2048 + (ty + 1) * 512], r=wkeys[("wi", l)], w=[kb])
                    DMA("sp", rc[:], rot_in[ty, 0], w=["rc"])
                    DMA("sp", rs[:], rot_in[ty, 1], w=["rs"])
                    for c in range(4):
                        q = qo[nq % 2]
                        kq = f"qo{nq % 2}"
                        nq += 1
                        for tt in range(NTT):
                            pka, pa = mmring.next()
                            pkb, pb = mmring.next()
                            for k in range(8):
                                A("pe", (lambda pa=pa, wa=wa, k=k, c=c, tt=tt: lambda E: E.matmul(pa[:], lhsT=wa[:, k, c * 128:(c + 1) * 128], rhs=xT[:, k, tt * TT:(tt + 1) * TT], start=(k == 0), stop=(k == 7)))(),
                                  r=[ka, ("xT", tt)], w=[pka])
                            for k in range(8):
                                A("pe", (lambda pb=pb, wb_=wb_, k=k, c=c, tt=tt: lambda E: E.matmul(pb[:], lhsT=wb_[:, k, c * 128:(c + 1) * 128], rhs=xT[:, k, tt * TT:(tt + 1) * TT], start=(k == 0), stop=(k == 7)))(),
                                  r=[kb, ("xT", tt)], w=[pkb])
                            a1, a2 = t1[tt % 2], t2[tt % 2]
                            k1, k2 = f"t1_{tt % 2}", f"t2_{tt % 2}"
                            A("dve", (lambda a1=a1, pa=pa, tt=tt: lambda E: E.tensor_tensor(a1[:], pa[:], rc[:, tt * TT:(tt + 1) * TT], ALU.mult))(), r=[pka, "rc"], w=[k1])
                            A("dve", (lambda a2=a2, pb=pb, tt=tt: lambda E: E.tensor_tensor(a2[:], pb[:], rs[:, tt * TT:(tt + 1) * TT], ALU.mult))(), r=[pkb, "rs"], w=[k2])
                            A("pool", (lambda q=q, a1=a1, a2=a2, tt=tt: lambda E: E.tensor_tensor(q[:, tt * TT:(tt + 1) * TT], a1[:], a2[:], ALU.add))(), r=[k1, k2], w=[kq])
                        ch = ty * 4 + c
                        DMA("pool", qk_d[ch], q[:], r=[kq], w=[("qk", ch)])
                wa = wA[0]
                DMA("sp", wa[:], wl[:, :, 4096:4608], r=wkeys[("wi", l)], w=["wA0"])
                wb_ = wA[1]
                DMA("sp", wb_[:], wl[:, :, 4608:5120], r=wkeys[("wi", l)], w=["wA1"])
                for qi, (wt, kw, sc) in enumerate(((wa, "wA0", 0.125), (wb_, "wA1", 1.0))):
                    for c in range(4):
                        q = qo[nq % 2]
                        kq = f"qo{nq % 2}"
                        nq += 1
                        for tt in range(NTT):
                            pka, pa = mmring.next()
                            for k in range(8):
                                A("pe", (lambda pa=pa, wt=wt, k=k, c=c, tt=tt: lambda E: E.matmul(pa[:], lhsT=wt[:, k, c * 128:(c + 1) * 128], rhs=xT[:, k, tt * TT:(tt + 1) * TT], start=(k == 0), stop=(k == 7)))(),
                                  r=[kw, ("xT", tt)], w=[pka])
                            A("act", (lambda q=q, pa=pa, tt=tt, sc=sc: lambda E: E.activation(q[:, tt * TT:(tt + 1) * TT], pa[:], AF.Identity, scale=sc))(), r=[pka], w=[kq])
                        ch = 16 + qi * 4 + c
                        DMA("pool", qk_d[ch], q[:], r=[kq], w=[("qk", ch)])
                nb = 0
                for g in range(6):
                    wt, kw = wB[g % 2], f"wB{g % 2}"
                    DMA("sp", wt[:], wl[:, :, 5120 + g * 512:5120 + (g + 1) * 512], r=wkeys[("wi", l)], w=[kw])
                    for ts in range(16):
                        pka, pa = mmring.next()
                        for k in range(8):
                            A("pe", (lambda pa=pa, wt=wt, k=k, ts=ts: lambda E: E.matmul(pa[:], lhsT=xT[:, k, ts * 128:(ts + 1) * 128], rhs=wt[:, k, :], start=(k == 0), stop=(k == 7)))(),
                              r=[kw, ("xT", ts // 4)], w=[pka])
                        if g < 4:
                            vbuf, kv = vb[nb % 2], f"vb{nb % 2}"
                            A("act", (lambda vbuf=vbuf, pa=pa, ts=ts: lambda E: E.copy(vbuf[:, ts % 4, :], pa[:]))(), r=[pka], w=[kv])
                            if ts % 4 == 3:
                                t0 = (ts // 4) * 512
                                DMA("pool", v_d[t0:t0 + 512, g * 512:(g + 1) * 512].rearrange("(s p) f -> p s f", p=128), vbuf[:], r=[kv], w=[("v", g, ts // 4)])
                                nb += 1
                        else:
                            sgt, ksg = sg[ts % 2], f"sg{ts % 2}"
                            rbuf, kr = rgb[nb % 2], f"rgb{nb % 2}"
                            A("act", (lambda sgt=sgt, pa=pa: lambda E: E.activation(sgt[:], pa[:], AF.Silu))(), r=[pka], w=[ksg])
                            A("pool", (lambda rbuf=rbuf, sgt=sgt, ts=ts: lambda E: E.tensor_tensor(rbuf[:, ts % 4, :], sgt[:], bct[:, 0:512], ALU.mult))(), r=[ksg, "bct"], w=[kr])
                            if ts % 4 == 3:
                                t0 = (ts // 4) * 512
                                DMA("pool", rg_d[t0:t0 + 512, (g - 4) * 512:(g - 3) * 512].rearrange("(s p) f -> p s f", p=128), rbuf[:], r=[kr], w=[("rg", g - 4, ts // 4)])
                                nb += 1

        string = Ring([(f"ps{i}", psb[i]) for i in range(3)])

        def rstd_from_ss(ssk, ss_ap, n, out_ap, outk):
            A("act", lambda E: E.activation(out_ap, ss_ap, AF.Ln, bias=EPS, scale=1.0 / n), r=[ssk], w=[outk])
            A("act", lambda E: E.activation(out_ap, out_ap, AF.Exp, scale=-0.5), r=[outk], w=[outk])

        LAG = 3
        NPT = 6

        def run_pipeline(items, front, back):
            n = len(items)
            for i in range(n + LAG):
                if i < n:
                    front(items[i])
                if i - LAG >= 0:
                    back(items[i - LAG])

        def phase_M_ret(l):
            with ExitStack() as st:
                qT = [sbuf(st, f"mq{i}", [128, S], BF16) for i in range(2)]
                kT = [sbuf(st, f"mk{i}", [128, S], BF16) for i in range(2)]
                vh = [sbuf(st, f"mv{i}", [128, 16, 256], BF16) for i in range(2)]
                dec = [sbuf(st, f"mdec{i}", [128, 5, 512], F32) for i in range(2)]
                rgt = [sbuf(st, f"mrg{i}", [128, 16, 256], F32) for i in range(2)]
                pt = [sbuf(st, f"mpt{i}", [128, 512], BF16) for i in range(NPT)]
                junk = sbuf(st, "mjunk", [128, 256], F32)
                ssb = [sbuf(st, f"mss{i}", [128, 2], F32) for i in range(4)]
                on = [sbuf(st, f"mon{i}", [128, 256], BF16) for i in range(2)]
                oTa = [sbuf(st, f"moT{i}", [128, 1024], BF16) for i in range(2)]
                accs = [(f"ps{3 + ns}", psb[3 + ns][:, 0:256]) for ns in range(4)]
                hk_ = {}

                def load_head(h):
                    b = h % 2
                    DMA("sp", qT[b][:], qk_d[h], r=[("qk", h)], w=[f"mq{b}"])
                    DMA("sp", kT[b][:], qk_d[4 + h], r=[("qk", 4 + h)], w=[f"mk{b}"])
                    kmv = DMA2("sp", vh[b][:], v_d[:, h * 256:(h + 1) * 256].rearrange("(s p) e -> p s e", p=128),
                               [("v", h // 2, i) for i in range(4)], f"mv{b}")
                    kmrg = DMA2("sp", rgt[b][:], rg_d[:, h * 256:(h + 1) * 256].rearrange("(s p) e -> p s e", p=128),
                                [("rg", h // 2, i) for i in range(4)], f"mrg{b}")
                    DMA("sp", dec[b][:], dec_in[h].rearrange("k p j -> p k j"), w=[f"mdec{b}"])
                    hk_[h] = (kmv, kmrg)

                items = []
                for h in range(4):
                    for nt in range(4):
                        for mt in range(4 * nt + 4):
                            items.append(dict(h=h, nt=nt, mt=mt, first=(nt == 0 and mt == 0), lastmt=(mt == 4 * nt + 3)))
                cnt = [0, 0]
                load_head(0)

                def front(it):
                    h, nt, mt = it["h"], it["nt"], it["mt"]
                    b = h % 2
                    delta = 512 * nt - 128 * mt
                    sk, sp_ = string.next()
                    A("pe", lambda E: E.matmul(sp_[:], lhsT=kT[b][:, mt * 128:(mt + 1) * 128], rhs=qT[b][:, nt * 512:(nt + 1) * 512], start=True, stop=True),
                      r=[f"mq{b}", f"mk{b}"], w=[sk])
                    p_, kp = pt[cnt[0] % NPT], f"mpt{cnt[0] % NPT}"
                    cnt[0] += 1
                    it["p"] = (p_, kp)
                    if delta >= 128:
                        scl = _ret_scalar(h, delta)
                        A("dve", lambda E: E.scalar_tensor_tensor(p_[:], sp_[:], scl, dec[b][:, 0, :], ALU.mult, ALU.mult), r=[sk, f"mdec{b}"], w=[kp])
                    else:
                        dd = (-delta) // 128
                        A("dve", lambda E: E.tensor_tensor(p_[:], sp_[:], dec[b][:, 1 + dd, :], ALU.mult), r=[sk, f"mdec{b}"], w=[kp])

                def back(it):
                    h, nt, mt = it["h"], it["nt"], it["mt"]
                    b = h % 2
                    p_, kp = it["p"]
                    kmv, kmrg = hk_[h]
                    if it["first"] and h + 1 < 4:
                        load_head(h + 1)
                    for ns in range(4):
                        last = 4 * nt + ns
                        if mt > last:
                            continue
                        ak, aap = accs[ns]
                        A("pe", (lambda aap=aap, ns=ns, last=last: lambda E: E.matmul(aap, lhsT=p_[:, ns * 128:(ns + 1) * 128], rhs=vh[b][:, mt, :], start=(mt == 0), stop=(mt == last)))(),
                          r=[kp, f"mv{b}"] + kmv, w=[ak])
                    if not it["lastmt"]:
                        return
                    ob, kob = oTa[cnt[1] % 2], f"moT{cnt[1] % 2}"
                    cnt[1] += 1
                    for ns in range(4):
                        ak, aap = accs[ns]
                        ss, kss = ssb[ns], f"mss{ns}"
                        o_, ko = on[ns % 2], f"mon{ns % 2}"
                        A("act", (lambda aap=aap, ss=ss: lambda E: E.activation(junk[:], aap, AF.Square, accum_out=ss[:, 0:1]))(), r=[ak], w=["mjunk", kss + "a"])
                        rstd_from_ss(kss + "a", ss[:, 0:1], 256.0, ss[:, 1:2], kss + "b")
                        A("dve", (lambda o_=o_, aap=aap, ss=ss, ns=ns: lambda E: E.scalar_tensor_tensor(o_[:], aap, ss[:, 1:2], rgt[b][:, nt * 4 + ns, :], ALU.mult, ALU.mult))(),
                          r=[ak, kss + "b", f"mrg{b}"] + kmrg, w=[ko])
                        for ec in range(2):
                            A("pe", (lambda o_=o_, ec=ec, ns=ns: lambda E: E.transpose(ps7[:, ec * 512 + ns * 128:ec * 512 + (ns + 1) * 128], o_[:, ec * 128:(ec + 1) * 128], idb[:]))(),
                              r=[ko, "idb"], w=["ps7"])
                    A("act", lambda E: E.copy(ob[:], ps7[:, 0:1024]), r=["ps7"], w=[kob])
                    for ec in range(2):
                        DMA("pool", oT_d[2 * h + ec, :, nt * 512:(nt + 1) * 512], ob[:, ec * 512:(ec + 1) * 512], r=[kob], w=[("oT", 2 * h + ec, nt)])

                run_pipeline(items, front, back)

        def phase_M_ca(l):
            with ExitStack() as st:
                qT = [sbuf(st, f"cq{i}", [128, S], BF16) for i in range(2)]
                kT = [sbuf(st, f"ck{i}", [128, S], BF16) for i in range(2)]
                lv = [sbuf(st, f"clv{i}", [128, 16, 128], BF16) for i in range(4)]
                hk = [sbuf(st, f"chk{i}", [128, 8, 512], F32) for i in range(1)]
                bm32 = sbuf(st, "cbm32", [128, 8, 512], F32)
                bhi = [sbuf(st, f"cbhi{i}", [128, 8, 512], BF16) for i in range(2)]
                blo = [sbuf(st, f"cblo{i}", [128, 8, 512], BF16) for i in range(2)]
                cam = sbuf(st, "ccam", [128, 8, 512], F32)
                pt = [sbuf(st, f"cpt{i}", [128, 512], BF16) for i in range(NPT)]
                rd = sbuf(st, "crd", [128, 512], F32)
                oc = [sbuf(st, f"coc{i}", [128, S], BF16) for i in range(2)]
                DMA("sp", cam[:], cam_in.rearrange("b p j -> p b j"), w=["ccam"])
                for i in range(4):
                    A("pool", (lambda i=i: lambda E: E.memset(lv[i][:], 1.0))(), w=[f"clv{i}"])
                hk_ = {}

                def load_head(h):
                    j, half = h // 2, h % 2
                    lo = 64 * half
                    if half == 0:
                        DMA("sp", qT[j % 2][:], qk_d[16 + j], r=[("qk", 16 + j)], w=[f"cq{j % 2}"])
                        DMA("sp", kT[j % 2][:], qk_d[20 + j], r=[("qk", 20 + j)], w=[f"ck{j % 2}"])
                    li = (j % 2) * 2 + half
                    klvs = DMA2("sp", lv[li][:, :, lo:lo + 64], v_d[:, 1024 + h * 64:1024 + (h + 1) * 64].rearrange("(s p) e -> p s e", p=128),
                                [("v", 2, i) for i in range(4)], f"clv{li}")
                    hb = h % 2
                    for b in range(8):
                        src = bass.AP(tensor=caext.tensor, offset=(l * 8 + h) * 1536 + 896 - 128 * b, ap=[[1, 128], [1, 512]])
                        DMA("sp", hk[0][:, b, :], src, w=[("chk", b)])
                    for b in range(8):
                        pk, pb = mmring.next()
                        A("pe", (lambda pb=pb, b=b: lambda E: E.matmul(pb[:], lhsT=antiI[:], rhs=hk[0][:, b, :], start=True, stop=True))(), r=[("chk", b), "antiI"], w=[pk])
                        A("dve", (lambda pb=pb, b=b: lambda E: E.tensor_tensor(bm32[:, b, :], pb[:], cam[:, b, :], ALU.add))(), r=[pk, "ccam"], w=[("cbm32", b)])
                        A("act", (lambda b=b: lambda E: E.copy(bhi[hb][:, b, :], bm32[:, b, :]))(), r=[("cbm32", b)], w=[("cbhi", hb, b)])
                        A("dve", (lambda b=b: lambda E: E.tensor_tensor(blo[hb][:, b, :], bm32[:, b, :], bhi[hb][:, b, :], ALU.subtract))(), r=[("cbm32", b), ("cbhi", hb, b)], w=[("cblo", hb, b)])
                    hk_[h] = klvs

                items = []
                for h in range(8):
                    for nt in range(4):
                        bs = [b for b in range(8) if 4 * nt - 4 + b >= 0]
                        for b in bs:
                            items.append(dict(h=h, nt=nt, b=b, first=(nt == 0 and b == bs[0]), b0=bs[0], lastb=(b == bs[-1])))
                cnt = [0, 0]
                load_head(0)

                def front(it):
                    h, nt, b = it["h"], it["nt"], it["b"]
                    j, half = h // 2, h % 2
                    lo = 64 * half
                    mt = 4 * nt - 4 + b
                    sk, sp_ = string.next()
                    A("pe", lambda E: E.matmul(sp_[:], lhsT=kT[j % 2][lo:lo + 64, mt * 128:(mt + 1) * 128], rhs=qT[j % 2][lo:lo + 64, nt * 512:(nt + 1) * 512], start=True, stop=False),
                      r=[f"cq{j % 2}", f"ck{j % 2}"], w=[sk])
                    A("pe", lambda E: E.matmul(sp_[:], lhsT=idb[:], rhs=bhi[h % 2][:, b, :], start=False, stop=False), r=["idb", ("cbhi", h % 2, b)], w=[sk])
                    A("pe", lambda E: E.matmul(sp_[:], lhsT=idb[:], rhs=blo[h % 2][:, b, :], start=False, stop=True), r=["idb", ("cblo", h % 2, b)], w=[sk])
                    p_, kp = pt[cnt[0] % NPT], f"cpt{cnt[0] % NPT}"
                    cnt[0] += 1
                    it["p"] = (p_, kp)
                    A("act", lambda E: E.activation(p_[:], sp_[:], AF.Exp), r=[sk], w=[kp])

                def back(it):
                    h, nt, b = it["h"], it["nt"], it["b"]
                    j, half = h // 2, h % 2
                    lo = 64 * half
                    li = (j % 2) * 2 + half
                    mt = 4 * nt - 4 + b
                    p_, kp = it["p"]
                    klvs = hk_[h]
                    if it["first"] and h + 1 < 8:
                        load_head(h + 1)
                    ok_, ob = "ps4", psb[4]
                    A("pe", lambda E: E.matmul(ob[:], lhsT=lv[li][:, mt, :], rhs=p_[:], start=(b == it["b0"]), stop=it["lastb"]),
                      r=[kp, f"clv{li}"] + klvs, w=[ok_])
                    if not it["lastb"]:
                        return
                    ocb, kocb = oc[j % 2], f"coc{j % 2}"
                    dl = 64 - lo
                    A("dve", lambda E: E.tensor_copy(rd[lo:lo + 64, :], ob[dl:dl + 64, :]), r=[ok_], w=["crd"])
                    A("dve", lambda E: E.reciprocal(rd[lo:lo + 64, :], rd[lo:lo + 64, :]), r=["crd"], w=["crd"])
                    A("dve", lambda E: E.tensor_tensor(ocb[lo:lo + 64, nt * 512:(nt + 1) * 512], ob[lo:lo + 64, :], rd[lo:lo + 64, :], ALU.mult),
                      r=[ok_, "crd"], w=[kocb])
                    if half == 1 and nt == 3:
                        DMA("pool", oT_d[8 + j], ocb[:], r=[kocb], w=[("oT", 8 + j, i) for i in range(4)])

                run_pipeline(items, front, back)

        def phase_M_da(l):
            with ExitStack() as st:
                qT = [sbuf(st, f"dq{i}", [128, S], BF16) for i in range(2)]
                kT = [sbuf(st, f"dk{i}", [128, S], BF16) for i in range(2)]
                va = [sbuf(st, f"dva{i}", [128, 16, 129], BF16) for i in range(2)]
                dam = sbuf(st, "ddam", [128, 4, 512], F32)
                tmp = [sbuf(st, f"dtmp{i}", [128, 512], F32) for i in range(2)]
                pt = [sbuf(st, f"dpt{i}", [128, 512], BF16) for i in range(NPT)]
                r_ = [sbuf(st, f"dr{i}", [128, 4], F32) for i in range(4)]
                u0 = [sbuf(st, f"du{i}", [128, 128], F32) for i in range(4)]
                dd_ = [sbuf(st, f"dd{i}", [128, 128], F32) for i in range(2)]
                junk = sbuf(st, "djunk", [128, 128], F32)
                on = [sbuf(st, f"don{i}", [128, 128], BF16) for i in range(2)]
                oc = [sbuf(st, f"doc{i}", [128, 512], BF16) for i in range(2)]
                accs = [(f"ps{3 + ns}", psb[3 + ns][:, 0:129]) for ns in range(4)]
                DMA("sp", dam[:], dam_in.rearrange("k p j -> p k j"), w=["ddam"])
                for i in range(2):
                    A("pool", (lambda i=i: lambda E: E.memset(va[i][:], 1.0))(), w=[f"dva{i}"])
                hk_ = {}

                def load_head(h):
                    b = h % 2
                    DMA("sp", qT[b][:], qk_d[8 + h], r=[("qk", 8 + h)], w=[f"dq{b}"])
                    DMA("sp", kT[b][:], qk_d[12 + h], r=[("qk", 12 + h)], w=[f"dk{b}"])
                    hk_[h] = DMA2("sp", va[b][:, :, 0:128], v_d[:, 1536 + h * 128:1536 + (h + 1) * 128].rearrange("(s p) e -> p s e", p=128),
                                  [("v", 3, i) for i in range(4)], f"dva{b}")

                items = []
                for h in range(4):
                    for nt in range(4):
                        for g in range(2):
                            for mt in range(4 * nt + 4):
                                items.append(dict(h=h, nt=nt, g=g, mt=mt, first=(nt == 0 and g == 0 and mt == 0), lastmt=(mt == 4 * nt + 3)))
                cnt = [0, 0, 0]
                load_head(0)

                def front(it):
                    h, nt, g, mt = it["h"], it["nt"], it["g"], it["mt"]
                    b = h % 2
                    lo = 64 * g
                    delta = 512 * nt - 128 * mt
                    sk, sp_ = string.next()
                    A("pe", lambda E: E.matmul(sp_[:], lhsT=kT[b][lo:lo + 64, mt * 128:(mt + 1) * 128], rhs=qT[b][lo:lo + 64, nt * 512:(nt + 1) * 512], start=True, stop=True),
                      r=[f"dq{b}", f"dk{b}"], w=[sk])
                    p_, kp = pt[cnt[0] % NPT], f"dpt{cnt[0] % NPT}"
                    cnt[0] += 1
                    it["p"] = (p_, kp)
                    if delta >= 128:
                        A("act", lambda E: E.activation(p_[:], sp_[:], AF.Exp), r=[sk], w=[kp])
                    else:
                        dd = (-delta) // 128
                        t_, kt = tmp[cnt[1] % 2], f"dtmp{cnt[1] % 2}"
                        cnt[1] += 1
                        A("dve", lambda E: E.tensor_tensor(t_[:], sp_[:], dam[:, dd, :], ALU.add), r=[sk, "ddam"], w=[kt])
                        A("act", lambda E: E.activation(p_[:], t_[:], AF.Exp), r=[kt], w=[kp])

                def back(it):
                    h, nt, g, mt = it["h"], it["nt"], it["g"], it["mt"]
                    b = h % 2
                    p_, kp = it["p"]
                    kdva = hk_[h]
                    if it["first"] and h + 1 < 4:
                        load_head(h + 1)
                    for ns in range(4):
                        last = 4 * nt + ns
                        if mt > last:
                            continue
                        ak, aap = accs[ns]
                        A("pe", (lambda aap=aap, ns=ns, last=last: lambda E: E.matmul(aap, lhsT=p_[:, ns * 128:(ns + 1) * 128], rhs=va[b][:, mt, :], start=(mt == 0), stop=(mt == last)))(),
                          r=[kp, f"dva{b}"] + kdva, w=[ak])
                    if not it["lastmt"]:
                        return
                    for ns in range(4):
                        ak, aap = accs[ns]
                        rr, kr = r_[ns], f"dr{ns}"
                        u_, ku = u0[ns], f"du{ns}"
                        if g == 0:
                            A("dve", (lambda rr=rr, aap=aap: lambda E: E.reciprocal(rr[:, 0:1], aap[:, 128:129]))(), r=[ak], w=[kr + "0"])
                            A("dve", (lambda u_=u_, aap=aap, rr=rr: lambda E: E.tensor_scalar_mul(u_[:], aap[:, 0:128], rr[:, 0:1]))(), r=[ak, kr + "0"], w=[ku])
                        else:
                            d_, kd = dd_[ns % 2], f"dd{ns % 2}"
                            o_, ko = on[ns % 2], f"don{ns % 2}"
                            A("dve", (lambda rr=rr, aap=aap: lambda E: E.reciprocal(rr[:, 1:2], aap[:, 128:129]))(), r=[ak], w=[kr + "1"])
                            A("dve", (lambda rr=rr: lambda E: E.tensor_tensor(rr[:, 1:2], rr[:, 1:2], nlam[:], ALU.mult))(), r=[kr + "1", "nlam"], w=[kr + "1"])
                            A("dve", (lambda d_=d_, aap=aap, rr=rr, u_=u_: lambda E: E.scalar_tensor_tensor(d_[:], aap[:, 0:128], rr[:, 1:2], u_[:], ALU.mult, ALU.add))(), r=[ak, kr + "1", ku], w=[kd])
                            A("act", (lambda d_=d_, rr=rr: lambda E: E.activation(junk[:], d_[:], AF.Square, accum_out=rr[:, 2:3]))(), r=[kd], w=["djunk", kr + "2"])
                            rstd_from_ss(kr + "2", rr[:, 2:3], 128.0, rr[:, 3:4], kr + "3")
                            A("dve", (lambda o_=o_, d_=d_, rr=rr: lambda E: E.scalar_tensor_tensor(o_[:], d_[:], rr[:, 3:4], gda[:], ALU.mult, ALU.mult))(), r=[kd, kr + "3", "gda"], w=[ko])
                            A("pe", (lambda o_=o_, ns=ns: lambda E: E.transpose(ps7[:, ns * 128:(ns + 1) * 128], o_[:], idb[:]))(), r=[ko, "idb"], w=["ps7"])
                    if g == 1:
                        ob, kob = oc[cnt[2] % 2], f"doc{cnt[2] % 2}"
                        cnt[2] += 1
                        A("act", lambda E: E.copy(ob[:], ps7[:, 0:512]), r=["ps7"], w=[kob])
                        DMA("pool", oT_d[12 + h, :, nt * 512:(nt + 1) * 512], ob[:], r=[kob], w=[("oT", 12 + h, nt)])

                run_pipeline(items, front, back)

        def layer_norm(st_bufs, xr, kxr, gcol, bcol, bf_out, bf_key, mid_hook=None):
            sq, mean_sb, var_sb = st_bufs
            kall = [(kxr, c) for c in range(8)]
            A("pool", lambda E: E.tensor_tensor(sq[:], xr[:], xr[:], ALU.mult), r=kall, w=["sq"])
            for c in range(8):
                A("pe", (lambda c=c: lambda E: E.matmul(psb[4][:], lhsT=onesf[:], rhs=xr[:, c, :], start=(c == 0), stop=(c == 7)))(), r=[(kxr, c), "onesf"], w=["ps4"])
            for c in range(8):
                A("pe", (lambda c=c: lambda E: E.matmul(psb[5][:], lhsT=onesf[:], rhs=sq[:, c, :], start=(c == 0), stop=(c == 7)))(), r=["sq", "onesf"], w=["ps5"])
            if mid_hook is not None:
                mid_hook()
            A("act", lambda E: E.copy(mean_sb[:], psb[4][:]), r=["ps4"], w=["mean"])
            A("pool", lambda E: E.tensor_tensor(var_sb[:], mean_sb[:], mean_sb[:], ALU.mult), r=["mean"], w=["var"])
            A("dve", lambda E: E.tensor_tensor(var_sb[:], psb[5][:], var_sb[:], ALU.subtract), r=["ps5", "var"], w=["var"])
            A("act", lambda E: E.activation(var_sb[:], var_sb[:], AF.Ln, bias=EPS, scale=1.0), r=["var"], w=["var"])
            A("act", lambda E: E.activation(var_sb[:], var_sb[:], AF.Exp, scale=-0.5), r=["var"], w=["var"])
            for c in range(8):
                A("dve", (lambda c=c: lambda E: E.tensor_tensor(xr[:, c, :], xr[:, c, :], mean_sb[:], ALU.subtract))(), r=[(kxr, c), "mean", "ps4"], w=[(kxr, c)])
                A("pool", (lambda c=c: lambda E: E.tensor_tensor(xr[:, c, :], xr[:, c, :], var_sb[:], ALU.mult))(), r=[(kxr, c), "var"], w=[(kxr, c)])
                A("act", (lambda c=c: lambda E: E.activation(xr[:, c, :], xr[:, c, :], AF.Identity, bias=ppt[:, bcol + c:bcol + c + 1], scale=ppt[:, gcol + c:gcol + c + 1]))(), r=[(kxr, c), "ppt"], w=[(kxr, c)])
                A("dve", (lambda c=c: lambda E: E.tensor_copy(bf_out(c), xr[:, c, :]))(), r=[(kxr, c)], w=[bf_key])

        def phase_T(sq_i, l, last_layer):
            with ExitStack() as st:
                xr = sbuf(st, "xr", [128, 8, TT], F32)
                bufA = sbuf(st, "bufA", [128, 22 * TT], BF16)
                bufB = sbuf(st, "bufB", [128, 8, TT], F32)
                wg = [sbuf(st, f"wg{i}", [128, 8, 384], BF16) for i in range(2)]
                wbr = [sbuf(st, f"wbr{i}", [128, 16, 128], BF16) for i in range(2)]
                gall = sbuf(st, "gall", [128, 24, TT], F32)
                mb_ = [sbuf(st, f"mb{i}", [128, TT], F32) for i in range(3)]
                merged = sbuf(st, "merged", [128, 8, TT], BF16)
                wo_ = [sbuf(st, f"wo{i}", [128, 8, 128], BF16) for i in range(2)]
                mean_sb = sbuf(st, "mean_sb", [128, TT], F32)
                var_sb = sbuf(st, "var_sb", [128, TT], F32)
                x1b = merged
                wfi = [sbuf(st, f"wfi{i}", [128, 8, 256], BF16) for i in range(2)]
                sgl = [sbuf(st, f"sgl{i}", [128, TT], F32) for i in range(2)]
                wfo = [sbuf(st, f"wfo{i}", [128, 22, 128], BF16) for i in range(2)]
                oTv = bufA[:, 0:16 * TT].rearrange("p (e t) -> p e t", t=TT)
                aTv = bufA[:, :].rearrange("p (e t) -> p e t", t=TT)
                wil = wi_b[l].rearrange("(c p) f -> p c f", p=128)
                wbl = wbr_b[l].rearrange("(e p) f -> p e f", p=128)
                wol = wo_b[l].rearrange("(c p) f -> p c f", p=128)
                wfil = wfi_b[l].rearrange("(c p) f -> p c f", p=128)
                wfol = wfo_b[l].rearrange("(j p) f -> p j f", p=128)
                def emit_gates(tg):
                    gsl = slice(tg * TT, (tg + 1) * TT)
                    for c in range(8):
                        wgt, kwg = wg[c % 2], f"wg{c % 2}"
                        DMA("sp", wgt[:], wil[:, :, 8192 + c * 384:8192 + (c + 1) * 384], r=wkeys[("wi", l)], w=[kwg])
                        for br in range(3):
                            pk, pb = mmring.next()
                            for k in range(8):
                                A("pe", (lambda pb=pb, wgt=wgt, k=k, br=br: lambda E: E.matmul(pb[:], lhsT=wgt[:, k, br * 128:(br + 1) * 128], rhs=xT[:, k, gsl], start=(k == 0), stop=(k == 7)))(),
                                  r=[kwg, ("xT", tg)], w=[pk])
                            A("act", (lambda pb=pb, br=br, c=c: lambda E: E.activation(gall[:, c * 3 + br, :], pb[:], AF.Sigmoid, bias=ppt[:, c * 3 + br:c * 3 + br + 1]))(), r=[pk, "ppt"], w=[("gall", c, br)])

                emit_gates(0)
                for tt in range(NTT):
                    tsl = slice(tt * TT, (tt + 1) * TT)
                    DMA("sp", xr[:], xres_d[:, :, tsl], r=[("xres", tt)], w=[("xr", c_) for c_ in range(8)])
                    kbA = DMA2("sp", oTv, oT_d[:, :, tsl].rearrange("e p t -> p e t"), [("oT", e, tt) for e in range(16)], "bufA")
                    for c in range(8):
                        wbt, kwb = wbr[c % 2], f"wbr{c % 2}"
                        DMA("sp", wbt[:], wbl[:, :, c * 128:(c + 1) * 128], r=wkeys[("wbr", l)], w=[kwb])
                        for br, (e0, e1) in enumerate(((0, 8), (8, 12), (12, 16))):
                            pk, pb = mmring.next()
                            for e in range(e0, e1):
                                A("pe", (lambda pb=pb, wbt=wbt, e=e, e0=e0, e1=e1: lambda E: E.matmul(pb[:], lhsT=wbt[:, e, :], rhs=oTv[:, e, :], start=(e == e0), stop=(e == e1 - 1)))(),
                                  r=[kwb, "bufA"] + kbA, w=[pk])
                            A("dve", (lambda pb=pb, br=br, c=c: lambda E: E.tensor_tensor(mb_[br][:], pb[:], gall[:, c * 3 + br, :], ALU.mult))(), r=[pk, ("gall", c, br)], w=[f"mb{br}"])
                        A("pool", lambda E: E.tensor_tensor(mb_[0][:], mb_[0][:], mb_[1][:], ALU.add), r=["mb0", "mb1"], w=["mb0"])
                        A("pool", (lambda c=c: lambda E: E.tensor_tensor(merged[:, c, :], mb_[0][:], mb_[2][:], ALU.add))(), r=["mb0", "mb2"], w=["merged"])
                    for c in range(8):
                        wt, kw = wo_[c % 2], f"wo{c % 2}"
                        DMA("sp", wt[:], wol[:, :, c * 128:(c + 1) * 128], r=wkeys[("wo", l)], w=[kw])
                        pk, pb = mmring.next()
                        for k in range(8):
                            A("pe", (lambda pb=pb, wt=wt, k=k: lambda E: E.matmul(pb[:], lhsT=wt[:, k, :], rhs=merged[:, k, :], start=(k == 0), stop=(k == 7)))(), r=[kw, "merged"], w=[pk])
                        A("dve", (lambda pb=pb, c=c: lambda E: E.scalar_tensor_tensor(xr[:, c, :], xr[:, c, :], ALPHA, pb[:], ALU.mult, ALU.add))(), r=[pk, ("xr", c)], w=[("xr", c)])
                    layer_norm((bufB, mean_sb, var_sb), xr, "xr", 24, 32, lambda c: x1b[:, c, :], "merged",
                               mid_hook=(lambda tt=tt: emit_gates(tt + 1)) if tt + 1 < NTT else None)
                    for j in range(22):
                        wt, kw = wfi[j % 2], f"wfi{j % 2}"
                        DMA("sp", wt[:], wfil[:, :, j * 256:(j + 1) * 256], r=wkeys[("wfi", l)], w=[kw])
                        pkg, pg = mmring.next()
                        pku, pu = mmring.next()
                        for k in range(8):
                            A("pe", (lambda pg=pg, wt=wt, k=k: lambda E: E.matmul(pg[:], lhsT=wt[:, k, 0:128], rhs=x1b[:, k, :], start=(k == 0), stop=(k == 7)))(), r=[kw, "merged"], w=[pkg])
                        for k in range(8):
                            A("pe", (lambda pu=pu, wt=wt, k=k: lambda E: E.matmul(pu[:], lhsT=wt[:, k, 128:256], rhs=x1b[:, k, :], start=(k == 0), stop=(k == 7)))(), r=[kw, "merged"], w=[pku])
                        sgt, ksg = sgl[j % 2], f"sgl{j % 2}"
                        A("act", (lambda sgt=sgt, pg=pg: lambda E: E.activation(sgt[:], pg[:], AF.Silu))(), r=[pkg], w=[ksg])
                        A("dve", (lambda sgt=sgt, pu=pu, j=j: lambda E: E.tensor_tensor(aTv[:, j, :], pu[:], sgt[:], ALU.mult))(), r=[pku, ksg], w=["bufA"])
                    for c in range(8):
                        wt, kw = wfo[c % 2], f"wfo{c % 2}"
                        DMA("sp", wt[:], wfol[:, :, c * 128:(c + 1) * 128], r=wkeys[("wfo", l)], w=[kw])
                        pk, pb = mmring.next()
                        for j in range(22):
                            A("pe", (lambda pb=pb, wt=wt, j=j: lambda E: E.matmul(pb[:], lhsT=wt[:, j, :], rhs=aTv[:, j, :], start=(j == 0), stop=(j == 21)))(), r=[kw, "bufA"], w=[pk])
                        A("dve", (lambda pb=pb, c=c: lambda E: E.scalar_tensor_tensor(xr[:, c, :], xr[:, c, :], ALPHA, pb[:], ALU.mult, ALU.add))(), r=[pk, ("xr", c)], w=[("xr", c)])
                    layer_norm((bufB, mean_sb, var_sb), xr, "xr", 40, 48, lambda c, tsl=tsl: xT[:, c, tsl], ("xT", tt))
                    if not last_layer:
                        DMA("pool", xres_d[:, :, tsl], xr[:], r=[("xr", c_) for c_ in range(8)], w=[("xres", tt)])
                    else:
                        ob = bufB
                        obv = bufB[:, :, :].rearrange("p a b -> p (a b)").rearrange("p (s d) -> p s d", d=D)
                        for s in range(4):
                            for cg in range(2):
                                pk, pb = mmring.next()
                                for cc in range(4):
                                    c = cg * 4 + cc
                                    A("pe", (lambda pb=pb, s=s, c=c, cc=cc: lambda E: E.transpose(pb[:, cc * 128:(cc + 1) * 128], xr[:, c, s * 128:(s + 1) * 128], idf[:]))(), r=[("xr", c), "idf"], w=[pk])
                                A("act", (lambda pb=pb, s=s, cg=cg: lambda E: E.copy(obv[:, s, cg * 512:(cg + 1) * 512], pb[:]))(), r=[pk], w=["sq"])
                        DMA("pool", out_d[sq_i, tsl, :].rearrange("(s p) d -> p s d", p=128), obv, r=["sq"], w=[("out", sq_i, tt)])

        stop = build_program.stop
        for sq_i in range(n_seq):
            if not _os.environ.get('SKIPIN'):
                phase_input(sq_i)
            SC.barrier()
            for l in range(n_layers):
                if stop < 1: break
                load_layer_params(l)
                phase_P(l)
                SC.barrier()
                if stop < 2: break
                if sq_i == 0 and l + 1 < n_layers:
                    convert_layer(l + 1)
                phase_M_ret(l)
                SC.barrier()
                if stop < 3: break
                phase_M_ca(l)
                SC.barrier()
                if stop < 4: break
                phase_M_da(l)
                SC.barrier()
                if stop < 5: break
                phase_T(sq_i, l, l == n_layers - 1)
                SC.barrier()
        SC.emit(sems, block)
        build_program.stats = SC.stats
        build_program.sched = SC
    return nc


build_program.stop = 9


def _prep_inputs(inp):
    w_in = np.asarray(inp["w_in"], np.float32)
    ar = np.arange
    ret_sw = np.concatenate([h * 128 + (ar(128) + 64) % 128 for h in range(4)])
    da_sw = np.concatenate([h * 64 + (ar(64) + 32) % 64 for h in range(8)])
    cols_R = np.concatenate([ar(0, 512), ar(512, 1024), ar(4608, 5120), ar(5120, 5632)])
    cols_RS = np.concatenate([ret_sw, 512 + ret_sw, 4608 + da_sw, 5120 + da_sw])
    cols_C = ar(3072, 4096)
    cols_V = np.concatenate([ar(1024, 2048), ar(4096, 4608), ar(5632, 6144), ar(2048, 3072)])
    cols_G = np.concatenate([6144 + br * 1024 + c * 128 + ar(128) for c in range(8) for br in range(3)])
    cols = np.concatenate([cols_R, cols_RS, cols_C, cols_V, cols_G])
    assert cols.shape[0] == WI_COLS
    wi = np.ascontiguousarray(w_in[:, :, cols])
    wbr = np.ascontiguousarray(np.concatenate([inp["w_branch_a"], inp["w_branch_b"], inp["w_branch_c"]], axis=1), np.float32)
    wfi_cols = np.concatenate([np.concatenate([j * 128 + ar(128), 2816 + j * 128 + ar(128)]) for j in range(22)])
    wfi = np.ascontiguousarray(np.asarray(inp["w_ffn_in"], np.float32)[:, :, wfi_cols])
    pp = np.zeros((L, 128, 56), np.float32)
    bm = np.asarray(inp["b_merge"], np.float32)
    for c in range(8):
        for br in range(3):
            pp[:, :, c * 3 + br] = bm[:, br * 1024 + c * 128:br * 1024 + (c + 1) * 128]
    for k, nm in enumerate(("ln1_g", "ln1_b", "ln2_g", "ln2_b")):
        v = np.asarray(inp[nm], np.float32).reshape(L, 8, 128)
        pp[:, :, 24 + 8 * k:32 + 8 * k] = np.transpose(v, (0, 2, 1))
    bc = np.zeros((L, 128, 896), np.float32)
    rg = np.asarray(inp["ret_norm_g"], np.float32)
    bc[:, :, 0:256] = rg[:, None, :]
    bc[:, :, 256:512] = rg[:, None, :]
    bc[:, :, 512:640] = np.asarray(inp["da_norm_g"], np.float32)[:, None, :]
    for k, nm in enumerate(("da_lambda_q1", "da_lambda_k1", "da_lambda_q2", "da_lambda_k2")):
        bc[:, :, 640 + 64 * k:704 + 64 * k] = np.asarray(inp[nm], np.float32)[:, None, :]
    u = np.arange(1536)
    idx = np.clip(u - 511, -256, 256) + 256
    caext = np.ascontiguousarray(np.asarray(inp["ca_rel_bias"], np.float32)[:, :, idx])
    shared = dict(wi=wi, wbr=wbr, wo=np.ascontiguousarray(inp["w_out"], np.float32), wfi=wfi,
                  wfo=np.ascontiguousarray(inp["w_ffn_out"], np.float32), pp=pp, bc=bc, caext=caext,
                  rot=_rot_tables(), dec=_ret_decay(), dam=_da_mask(), cam=_ca_mask())
    return shared


def kernel(**inputs):
    x = np.ascontiguousarray(np.asarray(inputs["x"], np.float32))
    B = x.shape[0]
    n_seq = B // N_CORES
    shared = _prep_inputs(inputs)
    nc = build_program(n_seq)
    in_maps = []
    for i in range(N_CORES):
        m = dict(shared)
        m["x"] = x[i * n_seq:(i + 1) * n_seq]
        in_maps.append(m)
    res = run_bass_kernel_spmd(nc, in_maps, core_ids=list(range(N_CORES)))
    out = np.concatenate([np.asarray(r["out"]) for r in res.results], axis=0)
    return out.astype(np.float32)
```

```python
import math
from contextlib import ExitStack
import numpy as np
import concourse.bass as bass
import concourse.mybir as mybir
from concourse.bass_utils import run_bass_kernel_spmd

F32 = mybir.dt.float32
BF16 = mybir.dt.bfloat16
ALU = mybir.AluOpType
AF = mybir.ActivationFunctionType
AX = mybir.AxisListType

S = 2048
D = 1024
L = 4
NTT = 4
TT = 512
ALPHA = (2.0 * L) ** 0.25
EPS = 1e-5
NEG = -30000.0
WI_COLS = 11264
N_CORES = 8


class Sched:
    ENGS = ("pe", "act", "dve", "pool", "sp")
    DMA_RING = 8

    def __init__(self, nc):
        self.nc = nc
        self.ops = []
        self.last_w = {}
        self.readers = {}
        self.dma_count = {e: 0 for e in self.ENGS}
        self.eng_last = {e: None for e in self.ENGS}
        self.dma_last = {}
        self.pending_barrier = {e: None for e in self.ENGS}

    def add(self, eng, fn, reads=(), writes=(), dma=False):
        idx = len(self.ops)
        def _isps(k):
            return isinstance(k, str) and k.startswith("ps") and k[2:3].isdigit()
        psk = [k.split("_")[0] for k in list(reads) + list(writes) if _isps(k)]
        reads = [k for k in reads if not _isps(k)]
        writes = [k for k in writes if not _isps(k)] + sorted(set(psk))
        raw = set()
        oth = set()
        for r in reads:
            if r in self.last_w:
                raw.add(self.last_w[r])
        for w in writes:
            if w in self.last_w:
                oth.add(self.last_w[w])
            oth.update(self.readers.get(w, ()))
        if self.pending_barrier[eng] is not None:
            oth.update(self.pending_barrier[eng])
            self.pending_barrier[eng] = None
        for r in reads:
            self.readers.setdefault(r, []).append(idx)
        for w in writes:
            self.last_w[w] = idx
            self.readers[w] = []
        raw.discard(idx)
        oth.discard(idx)
        best = {}
        keep_raw, keep_oth = set(), set()
        for d in raw | oth:
            dop = self.ops[d]
            if dop["dma"]:
                (keep_raw if d in raw else keep_oth).add(d)
                continue
            if dop["eng"] == eng and not dma:
                if d not in raw:
                    continue
            cur = best.get(dop["eng"])
            if cur is None or d > cur:
                best[dop["eng"]] = d
        for d in best.values():
            (keep_raw if d in raw else keep_oth).add(d)
        raw, oth = keep_raw, keep_oth
        op = dict(eng=eng, fn=fn, raw=raw, oth=oth - raw, dma=dma, sig=False, val=None)
        if dma:
            n = self.dma_count[eng]
            self.dma_count[eng] += 1
            op["slot"] = n % self.DMA_RING
            op["val"] = 16 * (n // self.DMA_RING + 1)
            self.dma_last[(eng, op["slot"])] = idx
        self.ops.append(op)
        self.eng_last[eng] = idx
        return idx

    def barrier(self):
        pts = set(v for v in self.eng_last.values() if v is not None)
        pts.update(self.dma_last.values())
        for e in self.ENGS:
            cur = self.pending_barrier[e] or set()
            self.pending_barrier[e] = set(cur) | pts

    def emit(self, sems, block):
        ops = self.ops
        for op in ops:
            for d in op["raw"] | op["oth"]:
                dop = ops[d]
                if dop["dma"]:
                    continue
                if dop["eng"] != op["eng"]:
                    dop["sig"] = True
                elif op["dma"]:
                    dop["sig"] = True
                elif d in op["raw"] and op["eng"] in ("act", "dve", "pool"):
                    dop["sig"] = True
        cnt = {e: 0 for e in self.ENGS}
        for op in ops:
            if not op["dma"] and op["sig"]:
                cnt[op["eng"]] += 1
                op["val"] = cnt[op["eng"]]
        per_eng = {e: [] for e in self.ENGS}
        for i, op in enumerate(ops):
            per_eng[op["eng"]].append(i)
        final_dma = {}
        for op in ops:
            if op["dma"]:
                final_dma[("dma", op["eng"], op["slot"])] = op["val"]
        self.nwaits = 0

        def run(e, E):
            waited = {}
            for i in per_eng[e]:
                op = ops[i]
                need = {}
                for d in op["raw"] | op["oth"]:
                    dop = ops[d]
                    if dop["dma"]:
                        key = ("dma", dop["eng"], dop["slot"])
                    else:
                        if not dop["sig"]:
                            continue
                        if dop["eng"] == e and not (d in op["raw"] or op["dma"]):
                            continue
                        if dop["eng"] == e and e == "pe":
                            continue
                        key = dop["eng"]
                    need[key] = max(need.get(key, 0), dop["val"])
                if op["dma"] and op["val"] > 16:
                    key = ("dma", e, op["slot"])
                    need[key] = max(need.get(key, 0), op["val"] - 16)
                op["waits"] = []
                for key, v in need.items():
                    if waited.get(key, 0) >= v:
                        continue
                    E.wait_ge(sems[key], v)
                    waited[key] = v
                    self.nwaits += 1
                    op["waits"].append((key, v))
                ins = op["fn"](E)
                if op["dma"]:
                    ins.then_inc(sems[("dma", e, op["slot"])], 16)
                elif op["sig"]:
                    ins.then_inc(sems[e], 1)
            if e == "sp":
                for key, v in final_dma.items():
                    E.wait_ge(sems[key], v)

        block.tensor(lambda E: run("pe", E))
        block.scalar(lambda E: run("act", E))
        block.vector(lambda E: run("dve", E))
        block.gpsimd(lambda E: run("pool", E))
        block.sync(lambda E: run("sp", E))
        self.stats = dict(nops=len(ops), nwaits=self.nwaits, sig=cnt)
        self.per_eng = per_eng


class Ring:
    def __init__(self, items):
        self.items = items
        self.i = 0

    def next(self):
        it = self.items[self.i % len(self.items)]
        self.i += 1
        return it


def _rot_tables():
    t = np.arange(S, dtype=np.float32)
    out = np.zeros((4, 2, 128, S), np.float32)
    p = np.arange(128)
    inv = (10000.0 ** (-np.arange(0, 128, 2, dtype=np.float32) / 128)).astype(np.float32)
    ang = (t[:, None] * inv[None, :]).astype(np.float32)
    cos = np.cos(ang).astype(np.float32).T
    sin = np.sin(ang).astype(np.float32).T
    fi = p % 64
    sgn = np.where(p < 64, -1.0, 1.0).astype(np.float32)
    for ty, sc in ((0, 1.0), (1, 128 ** -0.5)):
        out[ty, 0] = cos[fi] * np.float32(sc)
        out[ty, 1] = sin[fi] * sgn[:, None] * np.float32(sc)
    inv = (10000.0 ** (-np.arange(0, 64, 2, dtype=np.float32) / 64)).astype(np.float32)
    ang = (t[:, None] * inv[None, :]).astype(np.float32)
    cos = np.cos(ang).astype(np.float32).T
    sin = np.sin(ang).astype(np.float32).T
    pp = p % 64
    fi = pp % 32
    sgn = np.where(pp < 32, -1.0, 1.0).astype(np.float32)
    for ty, sc in ((2, 64 ** -0.5), (3, 1.0)):
        out[ty, 0] = cos[fi] * np.float32(sc)
        out[ty, 1] = sin[fi] * sgn[:, None] * np.float32(sc)
    return out


def _ret_decay():
    out = np.zeros((4, 5, 128, 512), np.float64)
    i = np.arange(128)[:, None]
    j = np.arange(512)[None, :]
    for h in range(4):
        lg = math.log1p(-(2.0 ** (-5.0 - h)))
        out[h, 0] = np.exp(lg * (128 + j - i))
        for dd in range(4):
            m = dd * 128 + i
            kc = m // 64
            qc = j // 64
            rel = j - m
            out[h, 1 + dd] = np.where(kc == qc, np.exp(lg * np.abs(rel)), np.where(kc < qc, np.exp(lg * rel), 0.0))
    return out.astype(np.float32)


def _da_mask():
    out = np.zeros((4, 128, 512), np.float32)
    i = np.arange(128)[:, None]
    j = np.arange(512)[None, :]
    for dd in range(4):
        kc = (dd * 128 + i) // 64
        qc = j // 64
        out[dd] = np.where(kc <= qc, 0.0, NEG)
    return out


def _ca_mask():
    out = np.zeros((8, 128, 512), np.float32)
    i = np.arange(128)[:, None]
    j = np.arange(512)[None, :]
    for b in range(8):
        kc = 2 * b + i // 64 - 8
        a = j // 64
        out[b] = np.where((kc >= a - 8) & (kc <= a), 0.0, NEG)
    return out


def _ret_scalar(h, delta):
    lg = math.log1p(-(2.0 ** (-5.0 - h)))
    return float(math.exp(lg * (delta - 128)))


def build_program(n_seq, n_layers=L, debug=False):
    nc = bass.Bass("TRN2", target_bir_lowering=False)
    dt_in = lambda name, shape, dt=F32: nc.dram_tensor(name, shape, dt, kind="ExternalInput").ap()
    def scr(name, shape, dt):
        kind = "ExternalOutput" if (debug and name in ("qk_d", "v_d", "rg_d", "oT_d")) else "Internal"
        return nc.dram_tensor(name, shape, dt, kind=kind).ap()
    x_in = dt_in("x", [n_seq, S, D])
    wi_f = dt_in("wi", [L, D, WI_COLS])
    wbr_f = dt_in("wbr", [L, 2048, D])
    wo_f = dt_in("wo", [L, D, D])
    wfi_f = dt_in("wfi", [L, D, 5632])
    wfo_f = dt_in("wfo", [L, 2816, D])
    pp_in = dt_in("pp", [L, 128, 56])
    bc_in = dt_in("bc", [L, 128, 896])
    caext = dt_in("caext", [L, 8, 1536])
    rot_in = dt_in("rot", [4, 2, 128, S])
    dec_in = dt_in("dec", [4, 5, 128, 512])
    dam_in = dt_in("dam", [4, 128, 512])
    cam_in = dt_in("cam", [8, 128, 512])
    out_d = nc.dram_tensor("out", [n_seq, S, D], F32, kind="ExternalOutput").ap()

    wi_b = scr("wi_b", [L, D, WI_COLS], BF16)
    wbr_b = scr("wbr_b", [L, 2048, D], BF16)
    wo_b = scr("wo_b", [L, D, D], BF16)
    wfi_b = scr("wfi_b", [L, D, 5632], BF16)
    wfo_b = scr("wfo_b", [L, 2816, D], BF16)
    xres_d = scr("xres_d", [128, 8, S], F32)
    qk_d = scr("qk_d", [24, 128, S], BF16)
    v_d = scr("v_d", [S, 2048], BF16)
    rg_d = scr("rg_d", [S, 1024], F32)
    oT_d = scr("oT_d", [16, 128, S], BF16)
    with ExitStack() as es:
        SC = Sched(nc)
        sems = {}
        for e in ("pe", "act", "dve", "pool"):
            sems[e] = es.enter_context(nc.semaphore("s_" + e))
        for q in ("sp", "pool", "act"):
            for k in range(Sched.DMA_RING):
                sems[("dma", q, k)] = es.enter_context(nc.semaphore(f"d_{q}{k}"))
        psb = [es.enter_context(nc.psum_tensor(f"ps{i}", [128, 512], F32)) for i in range(7)]
        ps7 = es.enter_context(nc.psum_tensor("ps7", [128, 1024], BF16))

        uniq = [0]

        def sbuf(stack, name, shape, dt):
            uniq[0] += 1
            return stack.enter_context(nc.sbuf_tensor(f"{name}_u{uniq[0]}", shape, dt))

        xT = sbuf(es, "xT", [128, 8, S], BF16)
        idb = sbuf(es, "idb", [128, 128], BF16)
        idf = sbuf(es, "idf", [128, 128], F32)
        antiI = sbuf(es, "antiI", [128, 128], F32)
        onesf = sbuf(es, "onesf", [128, 128], F32)
        ppt = sbuf(es, "ppt", [128, 56], F32)
        bct = sbuf(es, "bct", [128, 896], F32)
        gda = sbuf(es, "gda", [128, 128], F32)
        nlam = sbuf(es, "nlam", [128, 1], F32)
        sm = sbuf(es, "sm", [128, 8], F32)
        lp = sbuf(es, "lp", [128, 64], F32)

        block = es.enter_context(nc.Block())

        def A(eng, fn, r=(), w=()):
            SC.add(eng, fn, reads=r, writes=w)

        def DMA(q, out_ap, in_ap, r=(), w=()):
            SC.add(q, lambda E: E.dma_start(out=out_ap, in_=in_ap), reads=r, writes=w, dma=True)

        def DMA2(q, out_ap, in_ap, r, wkey):
            a = out_ap.shape[1]
            hlf = a // 2
            keys = [(wkey, 0), (wkey, 1)]
            DMA(q, out_ap[:, 0:hlf, :], in_ap[:, 0:hlf, :], r=r, w=[keys[0], wkey])
            DMA(q, out_ap[:, hlf:a, :], in_ap[:, hlf:a, :], r=r, w=[keys[1], wkey])
            return keys

        A("pool", lambda E: E.memset(idb[:], 0.0), w=["idb"])
        A("pool", lambda E: E.affine_select(out=idb[:], in_=idb[:], compare_op=ALU.not_equal, fill=1.0, base=0,
                                            pattern=[[-1, 128]], channel_multiplier=1), r=["idb"], w=["idb"])
        A("pool", lambda E: E.memset(idf[:], 0.0), w=["idf"])
        A("pool", lambda E: E.affine_select(out=idf[:], in_=idf[:], compare_op=ALU.not_equal, fill=1.0, base=0,
                                            pattern=[[-1, 128]], channel_multiplier=1), r=["idf"], w=["idf"])
        A("pool", lambda E: E.memset(antiI[:], 0.0), w=["antiI"])
        A("pool", lambda E: E.affine_select(out=antiI[:], in_=antiI[:], compare_op=ALU.not_equal, fill=1.0, base=-127,
                                            pattern=[[1, 128]], channel_multiplier=1), r=["antiI"], w=["antiI"])
        A("pool", lambda E: E.memset(onesf[:], 1.0 / D), w=["onesf"])
        wkeys = {}
        import os as _os
        for l in range(n_layers if not _os.environ.get('SKIPCONV') else 0):
            for (src, dst, rows, nm) in ((wi_f, wi_b, D, "wi"), (wbr_f, wbr_b, 2048, "wbr"), (wo_f, wo_b, D, "wo"),
                                         (wfi_f, wfi_b, D, "wfi"), (wfo_f, wfo_b, 2816, "wfo")):
                ncols = src.shape[2]
                for r0 in range(0, rows, 128):
                    for c0 in range(0, ncols, 2048):
                        c1 = min(ncols, c0 + 2048)
                        DMA("pool", dst[l, r0:r0 + 128, c0:c1], src[l, r0:r0 + 128, c0:c1], w=[(nm, l, r0, c0)])
                        wkeys.setdefault((nm, l), []).append((nm, l, r0, c0))

        mmring = Ring([(f"ps{i}", psb[i]) for i in range(4)])

        def load_layer_params(l):
            DMA("sp", ppt[:], pp_in[l], w=["ppt"])
            DMA("sp", bct[:], bc_in[l], w=["bct"])
            lam_init = 0.8 - 0.6 * math.exp(-0.3 * l)
            A("dve", lambda E: E.tensor_tensor(lp[:], bct[:, 640:704], bct[:, 704:768], ALU.mult), r=["bct"], w=["lp"])
            A("dve", lambda E: E.reduce_sum(sm[:, 0:1], lp[:], axis=AX.X), r=["lp"], w=["sm0"])
            A("dve", lambda E: E.tensor_tensor(lp[:], bct[:, 768:832], bct[:, 832:896], ALU.mult), r=["bct", "sm0"], w=["lp"])
            A("dve", lambda E: E.reduce_sum(sm[:, 1:2], lp[:], axis=AX.X), r=["lp"], w=["sm1"])
            A("act", lambda E: E.activation(sm[:, 2:3], sm[:, 0:1], AF.Exp), r=["sm0"], w=["sm2"])
            A("act", lambda E: E.activation(sm[:, 3:4], sm[:, 1:2], AF.Exp), r=["sm1"], w=["sm3"])
            A("dve", lambda E: E.tensor_tensor(sm[:, 4:5], sm[:, 2:3], sm[:, 3:4], ALU.subtract), r=["sm2", "sm3"], w=["sm4"])
            A("dve", lambda E: E.tensor_scalar(nlam[:], sm[:, 4:5], lam_init, -1.0, ALU.add, ALU.mult), r=["sm4"], w=["nlam"])
            A("dve", lambda E: E.tensor_scalar_mul(gda[:], bct[:, 512:640], 1.0 - lam_init), r=["bct"], w=["gda"])

        def phase_input(sq):
            with ExitStack() as st:
                xin = [sbuf(st, f"xin{i}", [128, 4, D], F32) for i in range(2)]
                xr = [sbuf(st, f"xri{i}", [128, 8, TT], F32) for i in range(2)]
                for tt in range(NTT):
                    xi = xin[tt % 2]
                    xo = xr[tt % 2]
                    ki, ko = f"xin{tt % 2}", f"xri{tt % 2}"
                    DMA("sp", xi[:], x_in[sq, tt * TT:(tt + 1) * TT, :].rearrange("(s p) d -> p s d", p=128), w=[ki])
                    for c in range(8):
                        pk, pb = mmring.next()
                        for s in range(4):
                            A("pe", (lambda pb=pb, xi=xi, s=s, c=c: lambda E: E.transpose(pb[:, s * 128:(s + 1) * 128], xi[:, s, c * 128:(c + 1) * 128], idf[:]))(),
                              r=[ki, "idf"], w=[pk])
                        A("act", (lambda pb=pb, xo=xo, c=c: lambda E: E.copy(xo[:, c, :], pb[:]))(), r=[pk], w=[ko])
                        A("dve", (lambda xo=xo, c=c, tt=tt: lambda E: E.tensor_copy(xT[:, c, tt * TT:(tt + 1) * TT], xo[:, c, :]))(), r=[ko], w=[("xT", tt)])
                    DMA("pool", xres_d[:, :, tt * TT:(tt + 1) * TT], xo[:], r=[ko], w=[("xres", tt)])

        def phase_P(l):
            with ExitStack() as st:
                wA = [sbuf(st, f"wA{i}", [128, 8, 512], BF16) for i in range(2)]
                wB = [sbuf(st, f"wB{i}", [128, 8, 512], BF16) for i in range(2)]
                rc = sbuf(st, "rc", [128, S], F32)
                rs = sbuf(st, "rs", [128, S], F32)
                t1 = [sbuf(st, f"t1_{i}", [128, 512], F32) for i in range(2)]
                t2 = [sbuf(st, f"t2_{i}", [128, 512], F32) for i in range(2)]
                qo = [sbuf(st, f"qo{i}", [128, S], BF16) for i in range(2)]
                vb = [sbuf(st, f"vb{i}", [128, 4, 512], BF16) for i in range(2)]
                sg = [sbuf(st, f"sg{i}", [128, 512], F32) for i in range(2)]
                rgb = [sbuf(st, f"rgb{i}", [128, 4, 512], F32) for i in range(2)]
                wl = wi_b[l].rearrange("(c p) f -> p c f", p=128)
                nq = 0
                for ty in range(4):
                    wa, wb_ = wA[ty % 2], wB[ty % 2]
                    ka, kb = f"wA{ty % 2}", f"wB{ty % 2}"
                    DMA("sp", wa[:], wl[:, :, ty * 512:(ty + 1) * 512], r=wkeys[("wi", l)], w=[ka])
                    DMA("sp", wb_[:], wl[:, :, 2048 + ty * 512:2048 + (ty + 1) * 512], r=wkeys[("wi", l)], w=[kb])
                    DMA("sp", rc[:], rot_in[ty, 0], w=["rc"])
                    DMA("sp", rs[:], rot_in[ty, 1], w=["rs"])
                    for c in range(4):
                        q = qo[nq % 2]
                        kq = f"qo{nq % 2}"
                        nq += 1
                        for tt in range(NTT):
                            pka, pa = mmring.next()
                            pkb, pb = mmring.next()
                            for k in range(8):
                                A("pe", (lambda pa=pa, wa=wa, k=k, c=c, tt=tt: lambda E: E.matmul(pa[:], lhsT=wa[:, k, c * 128:(c + 1) * 128], rhs=xT[:, k, tt * TT:(tt + 1) * TT], start=(k == 0), stop=(k == 7)))(),
                                  r=[ka, ("xT", tt)], w=[pka])
                            for k in range(8):
                                A("pe", (lambda pb=pb, wb_=wb_, k=k, c=c, tt=tt: lambda E: E.matmul(pb[:], lhsT=wb_[:, k, c * 128:(c + 1) * 128], rhs=xT[:, k, tt * TT:(tt + 1) * TT], start=(k == 0), stop=(k == 7)))(),
                                  r=[kb, ("xT", tt)], w=[pkb])
                            a1, a2 = t1[tt % 2], t2[tt % 2]
                            k1, k2 = f"t1_{tt % 2}", f"t2_{tt % 2}"
                            A("dve", (lambda a1=a1, pa=pa, tt=tt: lambda E: E.tensor_tensor(a1[:], pa[:], rc[:, tt * TT:(tt + 1) * TT], ALU.mult))(), r=[pka, "rc"], w=[k1])
                            A("dve", (lambda a2=a2, pb=pb, tt=tt: lambda E: E.tensor_tensor(a2[:], pb[:], rs[:, tt * TT:(tt + 1) * TT], ALU.mult))(), r=[pkb, "rs"], w=[k2])
                            A("pool", (lambda q=q, a1=a1, a2=a2, tt=tt: lambda E: E.tensor_tensor(q[:, tt * TT:(tt + 1) * TT], a1[:], a2[:], ALU.add))(), r=[k1, k2], w=[kq])
                        ch = ty * 4 + c
                        DMA("pool", qk_d[ch], q[:], r=[kq], w=[("qk", ch)])
                wa = wA[0]
                DMA("sp", wa[:], wl[:, :, 4096:4608], r=wkeys[("wi", l)], w=["wA0"])
                wb_ = wA[1]
                DMA("sp", wb_[:], wl[:, :, 4608:5120], r=wkeys[("wi", l)], w=["wA1"])
                for qi, (wt, kw, sc) in enumerate(((wa, "wA0", 0.125), (wb_, "wA1", 1.0))):
                    for c in range(4):
                        q = qo[nq % 2]
                        kq = f"qo{nq % 2}"
                        nq += 1
                        for tt in range(NTT):
                            pka, pa = mmring.next()
                            for k in range(8):
                                A("pe", (lambda pa=pa, wt=wt, k=k, c=c, tt=tt: lambda E: E.matmul(pa[:], lhsT=wt[:, k, c * 128:(c + 1) * 128], rhs=xT[:, k, tt * TT:(tt + 1) * TT], start=(k == 0), stop=(k == 7)))(),
                                  r=[kw, ("xT", tt)], w=[pka])
                            A("act", (lambda q=q, pa=pa, tt=tt, sc=sc: lambda E: E.activation(q[:, tt * TT:(tt + 1) * TT], pa[:], AF.Identity, scale=sc))(), r=[pka], w=[kq])
                        ch = 16 + qi * 4 + c
                        DMA("pool", qk_d[ch], q[:], r=[kq], w=[("qk", ch)])
                nb = 0
                for g in range(6):
                    wt, kw = wB[g % 2], f"wB{g % 2}"
                    DMA("sp", wt[:], wl[:, :, 5120 + g * 512:5120 + (g + 1) * 512], r=wkeys[("wi", l)], w=[kw])
                    for ts in range(16):
                        pka, pa = mmring.next()
                        for k in range(8):
                            A("pe", (lambda pa=pa, wt=wt, k=k, ts=ts: lambda E: E.matmul(pa[:], lhsT=xT[:, k, ts * 128:(ts + 1) * 128], rhs=wt[:, k, :], start=(k == 0), stop=(k == 7)))(),
                              r=[kw, ("xT", ts // 4)], w=[pka])
                        if g < 4:
                            vbuf, kv = vb[nb % 2], f"vb{nb % 2}"
                            A("act", (lambda vbuf=vbuf, pa=pa, ts=ts: lambda E: E.copy(vbuf[:, ts % 4, :], pa[:]))(), r=[pka], w=[kv])
                            if ts % 4 == 3:
                                t0 = (ts // 4) * 512
                                DMA("pool", v_d[t0:t0 + 512, g * 512:(g + 1) * 512].rearrange("(s p) f -> p s f", p=128), vbuf[:], r=[kv], w=[("v", g, ts // 4)])
                                nb += 1
                        else:
                            sgt, ksg = sg[ts % 2], f"sg{ts % 2}"
                            rbuf, kr = rgb[nb % 2], f"rgb{nb % 2}"
                            A("act", (lambda sgt=sgt, pa=pa: lambda E: E.activation(sgt[:], pa[:], AF.Silu))(), r=[pka], w=[ksg])
                            A("pool", (lambda rbuf=rbuf, sgt=sgt, ts=ts: lambda E: E.tensor_tensor(rbuf[:, ts % 4, :], sgt[:], bct[:, 0:512], ALU.mult))(), r=[ksg, "bct"], w=[kr])
                            if ts % 4 == 3:
                                t0 = (ts // 4) * 512
                                DMA("pool", rg_d[t0:t0 + 512, (g - 4) * 512:(g - 3) * 512].rearrange("(s p) f -> p s f", p=128), rbuf[:], r=[kr], w=[("rg", g - 4, ts // 4)])
                                nb += 1

        string = Ring([(f"ps{i}", psb[i]) for i in range(3)])

        def rstd_from_ss(ssk, ss_ap, n, out_ap, outk):
            A("act", lambda E: E.activation(out_ap, ss_ap, AF.Ln, bias=EPS, scale=1.0 / n), r=[ssk], w=[outk])
            A("act", lambda E: E.activation(out_ap, out_ap, AF.Exp, scale=-0.5), r=[outk], w=[outk])

        LAG = 3
        NPT = 6

        def run_pipeline(items, front, back):
            n = len(items)
            for i in range(n + LAG):
                if i < n:
                    front(items[i])
                if i - LAG >= 0:
                    back(items[i - LAG])

        def phase_M_ret(l):
            with ExitStack() as st:
                qT = [sbuf(st, f"mq{i}", [128, S], BF16) for i in range(2)]
                kT = [sbuf(st, f"mk{i}", [128, S], BF16) for i in range(2)]
                vh = [sbuf(st, f"mv{i}", [128, 16, 256], BF16) for i in range(2)]
                dec = [sbuf(st, f"mdec{i}", [128, 5, 512], F32) for i in range(2)]
                rgt = [sbuf(st, f"mrg{i}", [128, 16, 256], F32) for i in range(2)]
                pt = [sbuf(st, f"mpt{i}", [128, 512], BF16) for i in range(NPT)]
                junk = sbuf(st, "mjunk", [128, 256], F32)
                ssb = [sbuf(st, f"mss{i}", [128, 2], F32) for i in range(4)]
                on = [sbuf(st, f"mon{i}", [128, 256], BF16) for i in range(2)]
                oTa = [sbuf(st, f"moT{i}", [128, 1024], BF16) for i in range(2)]
                eo = [sbuf(st, f"meo{i}", [128, 256], F32) for i in range(8)]
                accs = [(f"ps{3 + ns}", psb[3 + ns][:, 0:256]) for ns in range(4)]
                hk_ = {}

                def load_head(h):
                    b = h % 2
                    DMA("sp", qT[b][:], qk_d[h], r=[("qk", h)], w=[f"mq{b}"])
                    DMA("sp", kT[b][:], qk_d[4 + h], r=[("qk", 4 + h)], w=[f"mk{b}"])
                    kmv = DMA2("sp", vh[b][:], v_d[:, h * 256:(h + 1) * 256].rearrange("(s p) e -> p s e", p=128),
                               [("v", h // 2, i) for i in range(4)], f"mv{b}")
                    kmrg = DMA2("sp", rgt[b][:], rg_d[:, h * 256:(h + 1) * 256].rearrange("(s p) e -> p s e", p=128),
                                [("rg", h // 2, i) for i in range(4)], f"mrg{b}")
                    DMA("sp", dec[b][:], dec_in[h].rearrange("k p j -> p k j"), w=[f"mdec{b}"])
                    hk_[h] = (kmv, kmrg)

                items = []
                for h in range(4):
                    for nt in range(4):
                        for mt in range(4 * nt + 4):
                            items.append(dict(h=h, nt=nt, mt=mt, first=(nt == 0 and mt == 0), lastmt=(mt == 4 * nt + 3)))
                cnt = [0, 0]
                load_head(0)

                def front(it):
                    h, nt, mt = it["h"], it["nt"], it["mt"]
                    b = h % 2
                    delta = 512 * nt - 128 * mt
                    sk, sp_ = string.next()
                    A("pe", lambda E: E.matmul(sp_[:], lhsT=kT[b][:, mt * 128:(mt + 1) * 128], rhs=qT[b][:, nt * 512:(nt + 1) * 512], start=True, stop=True),
                      r=[f"mq{b}", f"mk{b}"], w=[sk])
                    p_, kp = pt[cnt[0] % NPT], f"mpt{cnt[0] % NPT}"
                    cnt[0] += 1
                    it["p"] = (p_, kp)
                    if delta >= 128:
                        scl = _ret_scalar(h, delta)
                        A("dve", lambda E: E.scalar_tensor_tensor(p_[:], sp_[:], scl, dec[b][:, 0, :], ALU.mult, ALU.mult), r=[sk, f"mdec{b}"], w=[kp])
                    else:
                        dd = (-delta) // 128
                        A("dve", lambda E: E.tensor_tensor(p_[:], sp_[:], dec[b][:, 1 + dd, :], ALU.mult), r=[sk, f"mdec{b}"], w=[kp])

                def back(it):
                    h, nt, mt = it["h"], it["nt"], it["mt"]
                    b = h % 2
                    p_, kp = it["p"]
                    kmv, kmrg = hk_[h]
                    if it["first"] and h + 1 < 4:
                        load_head(h + 1)
                    for ns in range(4):
                        last = 4 * nt + ns
                        if mt > last:
                            continue
                        ak, aap = accs[ns]
                        A("pe", (lambda aap=aap, ns=ns, last=last: lambda E: E.matmul(aap, lhsT=p_[:, ns * 128:(ns + 1) * 128], rhs=vh[b][:, mt, :], start=(mt == 0), stop=(mt == last)))(),
                          r=[kp, f"mv{b}"] + kmv, w=[ak])
                    if not it["lastmt"]:
                        return
                    ob, kob = oTa[cnt[1] % 2], f"moT{cnt[1] % 2}"
                    ebase = (cnt[1] % 2) * 4
                    cnt[1] += 1
                    evs = []
                    for ns in range(4):
                        ak, aap = accs[ns]
                        e_, ke = eo[ebase + ns], f"meo{ebase + ns}"
                        A("dve", (lambda e_=e_, aap=aap: lambda E: E.tensor_copy(e_[:], aap))(), r=[ak], w=[ke])
                        evs.append((ke, e_[:]))
                    for ns in range(4):
                        ak, aap = evs[ns]
                        ss, kss = ssb[ns], f"mss{ns}"
                        o_, ko = on[ns % 2], f"mon{ns % 2}"
                        A("act", (lambda aap=aap, ss=ss: lambda E: E.activation(junk[:], aap, AF.Square, accum_out=ss[:, 0:1]))(), r=[ak], w=["mjunk", kss + "a"])
                        rstd_from_ss(kss + "a", ss[:, 0:1], 256.0, ss[:, 1:2], kss + "b")
                        A("dve", (lambda o_=o_, aap=aap, ss=ss, ns=ns: lambda E: E.scalar_tensor_tensor(o_[:], aap, ss[:, 1:2], rgt[b][:, nt * 4 + ns, :], ALU.mult, ALU.mult))(),
                          r=[ak, kss + "b", f"mrg{b}"] + kmrg, w=[ko])
                        for ec in range(2):
                            A("pe", (lambda o_=o_, ec=ec, ns=ns: lambda E: E.transpose(ps7[:, ec * 512 + ns * 128:ec * 512 + (ns + 1) * 128], o_[:, ec * 128:(ec + 1) * 128], idb[:]))(),
                              r=[ko, "idb"], w=["ps7"])
                    A("act", lambda E: E.copy(ob[:], ps7[:, 0:1024]), r=["ps7"], w=[kob])
                    for ec in range(2):
                        DMA("pool", oT_d[2 * h + ec, :, nt * 512:(nt + 1) * 512], ob[:, ec * 512:(ec + 1) * 512], r=[kob], w=[("oT", 2 * h + ec, nt)])

                run_pipeline(items, front, back)

        def phase_M_ca(l):
            with ExitStack() as st:
                qT = [sbuf(st, f"cq{i}", [128, S], BF16) for i in range(2)]
                kT = [sbuf(st, f"ck{i}", [128, S], BF16) for i in range(2)]
                lv = [sbuf(st, f"clv{i}", [128, 16, 128], BF16) for i in range(4)]
                hk = [sbuf(st, f"chk{i}", [128, 8, 512], F32) for i in range(2)]
                bm = [sbuf(st, f"cbm{i}", [128, 8, 512], F32) for i in range(2)]
                cam = sbuf(st, "ccam", [128, 8, 512], F32)
                tmp = [sbuf(st, f"ctmp{i}", [128, 512], F32) for i in range(3)]
                pt = [sbuf(st, f"cpt{i}", [128, 512], BF16) for i in range(NPT)]
                rds = [sbuf(st, f"crd{i}", [128, 512], F32) for i in range(2)]
                oc = [sbuf(st, f"coc{i}", [128, S], BF16) for i in range(2)]
                DMA("sp", cam[:], cam_in.rearrange("b p j -> p b j"), w=["ccam"])
                for i in range(4):
                    A("pool", (lambda i=i: lambda E: E.memset(lv[i][:], 1.0))(), w=[f"clv{i}"])
                hk_ = {}

                def load_head(h):
                    j, half = h // 2, h % 2
                    lo = 64 * half
                    if half == 0:
                        DMA("sp", qT[j % 2][:], qk_d[16 + j], r=[("qk", 16 + j)], w=[f"cq{j % 2}"])
                        DMA("sp", kT[j % 2][:], qk_d[20 + j], r=[("qk", 20 + j)], w=[f"ck{j % 2}"])
                    li = (j % 2) * 2 + half
                    klvs = DMA2("sp", lv[li][:, :, lo:lo + 64], v_d[:, 1024 + h * 64:1024 + (h + 1) * 64].rearrange("(s p) e -> p s e", p=128),
                                [("v", 2, i) for i in range(4)], f"clv{li}")
                    hb = h % 2
                    for b in range(8):
                        src = bass.AP(tensor=caext.tensor, offset=(l * 8 + h) * 1536 + 896 - 128 * b, ap=[[1, 128], [1, 512]])
                        DMA("sp", hk[hb][:, b, :], src, w=[("chk", hb, b)])
                    for b in range(8):
                        pk, pb = mmring.next()
                        A("pe", (lambda pb=pb, b=b: lambda E: E.matmul(pb[:], lhsT=antiI[:], rhs=hk[hb][:, b, :], start=True, stop=True))(), r=[("chk", hb, b), "antiI"], w=[pk])
                        A("dve", (lambda pb=pb, b=b: lambda E: E.tensor_tensor(bm[hb][:, b, :], pb[:], cam[:, b, :], ALU.add))(), r=[pk, "ccam"], w=[("cbm", hb, b)])
                    hk_[h] = klvs

                items = []
                for h in range(8):
                    for nt in range(4):
                        bs = [b for b in range(8) if 4 * nt - 4 + b >= 0]
                        for b in bs:
                            items.append(dict(h=h, nt=nt, b=b, first=(nt == 0 and b == bs[0]), b0=bs[0], lastb=(b == bs[-1])))
                cnt = [0, 0]
                load_head(0)

                def front(it):
                    h, nt, b = it["h"], it["nt"], it["b"]
                    j, half = h // 2, h % 2
                    lo = 64 * half
                    mt = 4 * nt - 4 + b
                    sk, sp_ = string.next()
                    A("pe", lambda E: E.matmul(sp_[:], lhsT=kT[j % 2][lo:lo + 64, mt * 128:(mt + 1) * 128], rhs=qT[j % 2][lo:lo + 64, nt * 512:(nt + 1) * 512], start=True, stop=True),
                      r=[f"cq{j % 2}", f"ck{j % 2}"], w=[sk])
                    t_, kt = tmp[cnt[0] % 3], f"ctmp{cnt[0] % 3}"
                    p_, kp = pt[cnt[0] % NPT], f"cpt{cnt[0] % NPT}"
                    cnt[0] += 1
                    it["p"] = (p_, kp)
                    A("dve", lambda E: E.tensor_tensor(t_[:], sp_[:], bm[h % 2][:, b, :], ALU.add), r=[sk, ("cbm", h % 2, b)], w=[kt])
                    A("act", lambda E: E.activation(p_[:], t_[:], AF.Exp), r=[kt], w=[kp])

                def back(it):
                    h, nt, b = it["h"], it["nt"], it["b"]
                    j, half = h // 2, h % 2
                    lo = 64 * half
                    li = (j % 2) * 2 + half
                    mt = 4 * nt - 4 + b
                    p_, kp = it["p"]
                    klvs = hk_[h]
                    if it["first"] and h + 1 < 8:
                        load_head(h + 1)
                    bi = 4 + (h * 4 + nt) % 2
                    ok_, ob = f"ps{bi}", psb[bi]
                    A("pe", lambda E: E.matmul(ob[:], lhsT=lv[li][:, mt, :], rhs=p_[:], start=(b == it["b0"]), stop=it["lastb"]),
                      r=[kp, f"clv{li}"] + klvs, w=[ok_])
                    if not it["lastb"]:
                        return
                    ocb, kocb = oc[j % 2], f"coc{j % 2}"
                    dl = 64 - lo
                    rd, krd = rds[bi - 4], f"crd{bi - 4}"
                    A("dve", lambda E: E.tensor_copy(rd[lo:lo + 64, :], ob[dl:dl + 64, :]), r=[ok_], w=[krd])
                    A("dve", lambda E: E.reciprocal(rd[lo:lo + 64, :], rd[lo:lo + 64, :]), r=[krd], w=[krd])
                    A("dve", lambda E: E.tensor_tensor(ocb[lo:lo + 64, nt * 512:(nt + 1) * 512], ob[lo:lo + 64, :], rd[lo:lo + 64, :], ALU.mult),
                      r=[ok_, krd], w=[kocb])
                    if half == 1 and nt == 3:
                        DMA("pool", oT_d[8 + j], ocb[:], r=[kocb], w=[("oT", 8 + j, i) for i in range(4)])

                run_pipeline(items, front, back)

        def phase_M_da(l):
            with ExitStack() as st:
                qT = [sbuf(st, f"dq{i}", [128, S], BF16) for i in range(2)]
                kT = [sbuf(st, f"dk{i}", [128, S], BF16) for i in range(2)]
                va = [sbuf(st, f"dva{i}", [128, 16, 129], BF16) for i in range(2)]
                dam = sbuf(st, "ddam", [128, 4, 512], F32)
                tmp = [sbuf(st, f"dtmp{i}", [128, 512], F32) for i in range(2)]
                pt = [sbuf(st, f"dpt{i}", [128, 512], BF16) for i in range(NPT)]
                r_ = [sbuf(st, f"dr{i}", [128, 4], F32) for i in range(4)]
                u0 = [sbuf(st, f"du{i}", [128, 128], F32) for i in range(4)]
                dd_ = [sbuf(st, f"dd{i}", [128, 128], F32) for i in range(2)]
                junk = sbuf(st, "djunk", [128, 128], F32)
                on = [sbuf(st, f"don{i}", [128, 128], BF16) for i in range(2)]
                oc = [sbuf(st, f"doc{i}", [128, 512], BF16) for i in range(2)]
                ea = [sbuf(st, f"dea{i}", [128, 129], F32) for i in range(8)]
                accs = [(f"ps{3 + ns}", psb[3 + ns][:, 0:129]) for ns in range(4)]
                DMA("sp", dam[:], dam_in.rearrange("k p j -> p k j"), w=["ddam"])
                for i in range(2):
                    A("pool", (lambda i=i: lambda E: E.memset(va[i][:], 1.0))(), w=[f"dva{i}"])
                hk_ = {}

                def load_head(h):
                    b = h % 2
                    DMA("sp", qT[b][:], qk_d[8 + h], r=[("qk", 8 + h)], w=[f"dq{b}"])
                    DMA("sp", kT[b][:], qk_d[12 + h], r=[("qk", 12 + h)], w=[f"dk{b}"])
                    hk_[h] = DMA2("sp", va[b][:, :, 0:128], v_d[:, 1536 + h * 128:1536 + (h + 1) * 128].rearrange("(s p) e -> p s e", p=128),
                                  [("v", 3, i) for i in range(4)], f"dva{b}")

                items = []
                for h in range(4):
                    for nt in range(4):
                        for g in range(2):
                            for mt in range(4 * nt + 4):
                                items.append(dict(h=h, nt=nt, g=g, mt=mt, first=(nt == 0 and g == 0 and mt == 0), lastmt=(mt == 4 * nt + 3)))
                cnt = [0, 0, 0]
                load_head(0)

                def front(it):
                    h, nt, g, mt = it["h"], it["nt"], it["g"], it["mt"]
                    b = h % 2
                    lo = 64 * g
                    delta = 512 * nt - 128 * mt
                    sk, sp_ = string.next()
                    A("pe", lambda E: E.matmul(sp_[:], lhsT=kT[b][lo:lo + 64, mt * 128:(mt + 1) * 128], rhs=qT[b][lo:lo + 64, nt * 512:(nt + 1) * 512], start=True, stop=True),
                      r=[f"dq{b}", f"dk{b}"], w=[sk])
                    p_, kp = pt[cnt[0] % NPT], f"dpt{cnt[0] % NPT}"
                    cnt[0] += 1
                    it["p"] = (p_, kp)
                    if delta >= 128:
                        A("act", lambda E: E.activation(p_[:], sp_[:], AF.Exp), r=[sk], w=[kp])
                    else:
                        dd = (-delta) // 128
                        t_, kt = tmp[cnt[1] % 2], f"dtmp{cnt[1] % 2}"
                        cnt[1] += 1
                        A("dve", lambda E: E.tensor_tensor(t_[:], sp_[:], dam[:, dd, :], ALU.add), r=[sk, "ddam"], w=[kt])
                        A("act", lambda E: E.activation(p_[:], t_[:], AF.Exp), r=[kt], w=[kp])

                def back(it):
                    h, nt, g, mt = it["h"], it["nt"], it["g"], it["mt"]
                    b = h % 2
                    p_, kp = it["p"]
                    kdva = hk_[h]
                    if it["first"] and h + 1 < 4:
                        load_head(h + 1)
                    for ns in range(4):
                        last = 4 * nt + ns
                        if mt > last:
                            continue
                        ak, aap = accs[ns]
                        A("pe", (lambda aap=aap, ns=ns, last=last: lambda E: E.matmul(aap, lhsT=p_[:, ns * 128:(ns + 1) * 128], rhs=va[b][:, mt, :], start=(mt == 0), stop=(mt == last)))(),
                          r=[kp, f"dva{b}"] + kdva, w=[ak])
                    if not it["lastmt"]:
                        return
                    ebase = g * 4
                    evs = []
                    for ns in range(4):
                        ak, aap = accs[ns]
                        e_, ke = ea[ebase + ns], f"dea{ebase + ns}"
                        A("dve", (lambda e_=e_, aap=aap: lambda E: E.tensor_copy(e_[:], aap))(), r=[ak], w=[ke])
                        evs.append((ke, e_[:]))
                    for ns in range(4):
                        ak, aap = evs[ns]
                        rr, kr = r_[ns], f"dr{ns}"
                        u_, ku = u0[ns], f"du{ns}"
                        if g == 0:
                            A("dve", (lambda rr=rr, aap=aap: lambda E: E.reciprocal(rr[:, 0:1], aap[:, 128:129]))(), r=[ak], w=[kr + "0"])
                            A("dve", (lambda u_=u_, aap=aap, rr=rr: lambda E: E.tensor_scalar_mul(u_[:], aap[:, 0:128], rr[:, 0:1]))(), r=[ak, kr + "0"], w=[ku])
                        else:
                            d_, kd = dd_[ns % 2], f"dd{ns % 2}"
                            o_, ko = on[ns % 2], f"don{ns % 2}"
                            A("dve", (lambda rr=rr, aap=aap: lambda E: E.reciprocal(rr[:, 1:2], aap[:, 128:129]))(), r=[ak], w=[kr + "1"])
                            A("dve", (lambda rr=rr: lambda E: E.tensor_tensor(rr[:, 1:2], rr[:, 1:2], nlam[:], ALU.mult))(), r=[kr + "1", "nlam"], w=[kr + "1"])
                            A("dve", (lambda d_=d_, aap=aap, rr=rr, u_=u_: lambda E: E.scalar_tensor_tensor(d_[:], aap[:, 0:128], rr[:, 1:2], u_[:], ALU.mult, ALU.add))(), r=[ak, kr + "1", ku], w=[kd])
                            A("act", (lambda d_=d_, rr=rr: lambda E: E.activation(junk[:], d_[:], AF.Square, accum_out=rr[:, 2:3]))(), r=[kd], w=["djunk", kr + "2"])
                            rstd_from_ss(kr + "2", rr[:, 2:3], 128.0, rr[:, 3:4], kr + "3")
                            A("dve", (lambda o_=o_, d_=d_, rr=rr: lambda E: E.scalar_tensor_tensor(o_[:], d_[:], rr[:, 3:4], gda[:], ALU.mult, ALU.mult))(), r=[kd, kr + "3", "gda"], w=[ko])
                            A("pe", (lambda o_=o_, ns=ns: lambda E: E.transpose(ps7[:, ns * 128:(ns + 1) * 128], o_[:], idb[:]))(), r=[ko, "idb"], w=["ps7"])
                    if g == 1:
                        ob, kob = oc[cnt[2] % 2], f"doc{cnt[2] % 2}"
                        cnt[2] += 1
                        A("act", lambda E: E.copy(ob[:], ps7[:, 0:512]), r=["ps7"], w=[kob])
                        DMA("pool", oT_d[12 + h, :, nt * 512:(nt + 1) * 512], ob[:], r=[kob], w=[("oT", 12 + h, nt)])

                run_pipeline(items, front, back)

        def layer_norm(st_bufs, xr, kxr, gcol, bcol, bf_out, bf_key):
            sq, mean_sb, var_sb = st_bufs
            kall = [(kxr, c) for c in range(8)]
            A("pool", lambda E: E.tensor_tensor(sq[:], xr[:], xr[:], ALU.mult), r=kall, w=["sq"])
            for c in range(8):
                A("pe", (lambda c=c: lambda E: E.matmul(psb[4][:], lhsT=onesf[:], rhs=xr[:, c, :], start=(c == 0), stop=(c == 7)))(), r=[(kxr, c), "onesf"], w=["ps4"])
            for c in range(8):
                A("pe", (lambda c=c: lambda E: E.matmul(psb[5][:], lhsT=onesf[:], rhs=sq[:, c, :], start=(c == 0), stop=(c == 7)))(), r=["sq", "onesf"], w=["ps5"])
            A("act", lambda E: E.copy(mean_sb[:], psb[4][:]), r=["ps4"], w=["mean"])
            A("pool", lambda E: E.tensor_tensor(var_sb[:], mean_sb[:], mean_sb[:], ALU.mult), r=["mean"], w=["var"])
            A("dve", lambda E: E.tensor_tensor(var_sb[:], psb[5][:], var_sb[:], ALU.subtract), r=["ps5", "var"], w=["var"])
            A("act", lambda E: E.activation(var_sb[:], var_sb[:], AF.Ln, bias=EPS, scale=1.0), r=["var"], w=["var"])
            A("act", lambda E: E.activation(var_sb[:], var_sb[:], AF.Exp, scale=-0.5), r=["var"], w=["var"])
            for c in range(8):
                A("dve", (lambda c=c: lambda E: E.tensor_tensor(xr[:, c, :], xr[:, c, :], mean_sb[:], ALU.subtract))(), r=[(kxr, c), "mean", "ps4"], w=[(kxr, c)])
                A("pool", (lambda c=c: lambda E: E.tensor_tensor(xr[:, c, :], xr[:, c, :], var_sb[:], ALU.mult))(), r=[(kxr, c), "var"], w=[(kxr, c)])
                A("act", (lambda c=c: lambda E: E.activation(xr[:, c, :], xr[:, c, :], AF.Identity, bias=ppt[:, bcol + c:bcol + c + 1], scale=ppt[:, gcol + c:gcol + c + 1]))(), r=[(kxr, c), "ppt"], w=[(kxr, c)])
                A("dve", (lambda c=c: lambda E: E.tensor_copy(bf_out(c), xr[:, c, :]))(), r=[(kxr, c)], w=[bf_key])

        def phase_T(sq_i, l, last_layer):
            with ExitStack() as st:
                xr = sbuf(st, "xr", [128, 8, TT], F32)
                bufA = sbuf(st, "bufA", [128, 22 * TT], BF16)
                bufB = sbuf(st, "bufB", [128, 8, TT], F32)
                wg = [sbuf(st, f"wg{i}", [128, 8, 384], BF16) for i in range(2)]
                wbr = [sbuf(st, f"wbr{i}", [128, 16, 128], BF16) for i in range(2)]
                gsb = [sbuf(st, f"gsb{i}", [128, TT], F32) for i in range(3)]
                mb_ = [sbuf(st, f"mb{i}", [128, TT], F32) for i in range(3)]
                merged = sbuf(st, "merged", [128, 8, TT], BF16)
                wo_ = [sbuf(st, f"wo{i}", [128, 8, 128], BF16) for i in range(2)]
                mean_sb = sbuf(st, "mean_sb", [128, TT], F32)
                var_sb = sbuf(st, "var_sb", [128, TT], F32)
                x1b = sbuf(st, "x1b", [128, 8, TT], BF16)
                wfi = [sbuf(st, f"wfi{i}", [128, 8, 256], BF16) for i in range(3)]
                sgl = [sbuf(st, f"sgl{i}", [128, TT], F32) for i in range(2)]
                wfo = [sbuf(st, f"wfo{i}", [128, 22, 128], BF16) for i in range(2)]
                oTv = bufA[:, 0:16 * TT].rearrange("p (e t) -> p e t", t=TT)
                aTv = bufA[:, :].rearrange("p (e t) -> p e t", t=TT)
                wil = wi_b[l].rearrange("(c p) f -> p c f", p=128)
                wbl = wbr_b[l].rearrange("(e p) f -> p e f", p=128)
                wol = wo_b[l].rearrange("(c p) f -> p c f", p=128)
                wfil = wfi_b[l].rearrange("(c p) f -> p c f", p=128)
                wfol = wfo_b[l].rearrange("(j p) f -> p j f", p=128)
                for tt in range(NTT):
                    tsl = slice(tt * TT, (tt + 1) * TT)
                    DMA("sp", xr[:], xres_d[:, :, tsl], r=[("xres", tt)], w=[("xr", c_) for c_ in range(8)])
                    kbA = DMA2("sp", oTv, oT_d[:, :, tsl].rearrange("e p t -> p e t"), [("oT", e, tt) for e in range(16)], "bufA")
                    for c in range(8):
                        wgt, kwg = wg[c % 2], f"wg{c % 2}"
                        wbt, kwb = wbr[c % 2], f"wbr{c % 2}"
                        DMA("sp", wgt[:], wil[:, :, 8192 + c * 384:8192 + (c + 1) * 384], r=wkeys[("wi", l)], w=[kwg])
                        DMA("sp", wbt[:], wbl[:, :, c * 128:(c + 1) * 128], r=wkeys[("wbr", l)], w=[kwb])
                        for br in range(3):
                            pk, pb = mmring.next()
                            for k in range(8):
                                A("pe", (lambda pb=pb, wgt=wgt, k=k, br=br, tsl=tsl: lambda E: E.matmul(pb[:], lhsT=wgt[:, k, br * 128:(br + 1) * 128], rhs=xT[:, k, tsl], start=(k == 0), stop=(k == 7)))(),
                                  r=[kwg, ("xT", tt)], w=[pk])
                            A("act", (lambda pb=pb, br=br, c=c: lambda E: E.activation(gsb[br][:], pb[:], AF.Sigmoid, bias=ppt[:, c * 3 + br:c * 3 + br + 1]))(), r=[pk, "ppt"], w=[f"gsb{br}"])
                        for br, (e0, e1) in enumerate(((0, 8), (8, 12), (12, 16))):
                            pk, pb = mmring.next()
                            for e in range(e0, e1):
                                A("pe", (lambda pb=pb, wbt=wbt, e=e, e0=e0, e1=e1: lambda E: E.matmul(pb[:], lhsT=wbt[:, e, :], rhs=oTv[:, e, :], start=(e == e0), stop=(e == e1 - 1)))(),
                                  r=[kwb, "bufA"] + kbA, w=[pk])
                            A("dve", (lambda pb=pb, br=br: lambda E: E.tensor_tensor(mb_[br][:], pb[:], gsb[br][:], ALU.mult))(), r=[pk, f"gsb{br}"], w=[f"mb{br}"])
                        A("pool", lambda E: E.tensor_tensor(mb_[0][:], mb_[0][:], mb_[1][:], ALU.add), r=["mb0", "mb1"], w=["mb0"])
                        A("pool", (lambda c=c: lambda E: E.tensor_tensor(merged[:, c, :], mb_[0][:], mb_[2][:], ALU.add))(), r=["mb0", "mb2"], w=["merged"])
                    for c in range(8):
                        wt, kw = wo_[c % 2], f"wo{c % 2}"
                        DMA("sp", wt[:], wol[:, :, c * 128:(c + 1) * 128], r=wkeys[("wo", l)], w=[kw])
                        pk, pb = mmring.next()
                        for k in range(8):
                            A("pe", (lambda pb=pb, wt=wt, k=k: lambda E: E.matmul(pb[:], lhsT=wt[:, k, :], rhs=merged[:, k, :], start=(k == 0), stop=(k == 7)))(), r=[kw, "merged"], w=[pk])
                        A("dve", (lambda pb=pb, c=c: lambda E: E.scalar_tensor_tensor(xr[:, c, :], xr[:, c, :], ALPHA, pb[:], ALU.mult, ALU.add))(), r=[pk, ("xr", c)], w=[("xr", c)])
                    layer_norm((bufB, mean_sb, var_sb), xr, "xr", 24, 32, lambda c: x1b[:, c, :], "x1b")
                    for j in range(22):
                        wt, kw = wfi[j % 3], f"wfi{j % 3}"
                        DMA("sp", wt[:], wfil[:, :, j * 256:(j + 1) * 256], r=wkeys[("wfi", l)], w=[kw])
                        pkg, pg = mmring.next()
                        pku, pu = mmring.next()
                        for k in range(8):
                            A("pe", (lambda pg=pg, wt=wt, k=k: lambda E: E.matmul(pg[:], lhsT=wt[:, k, 0:128], rhs=x1b[:, k, :], start=(k == 0), stop=(k == 7)))(), r=[kw, "x1b"], w=[pkg])
                        for k in range(8):
                            A("pe", (lambda pu=pu, wt=wt, k=k: lambda E: E.matmul(pu[:], lhsT=wt[:, k, 128:256], rhs=x1b[:, k, :], start=(k == 0), stop=(k == 7)))(), r=[kw, "x1b"], w=[pku])
                        sgt, ksg = sgl[j % 2], f"sgl{j % 2}"
                        A("act", (lambda sgt=sgt, pg=pg: lambda E: E.activation(sgt[:], pg[:], AF.Silu))(), r=[pkg], w=[ksg])
                        A("dve", (lambda sgt=sgt, pu=pu, j=j: lambda E: E.tensor_tensor(aTv[:, j, :], pu[:], sgt[:], ALU.mult))(), r=[pku, ksg], w=["bufA"])
                    for c in range(8):
                        wt, kw = wfo[c % 2], f"wfo{c % 2}"
                        DMA("sp", wt[:], wfol[:, :, c * 128:(c + 1) * 128], r=wkeys[("wfo", l)], w=[kw])
                        pk, pb = mmring.next()
                        for j in range(22):
                            A("pe", (lambda pb=pb, wt=wt, j=j: lambda E: E.matmul(pb[:], lhsT=wt[:, j, :], rhs=aTv[:, j, :], start=(j == 0), stop=(j == 21)))(), r=[kw, "bufA"], w=[pk])
                        A("dve", (lambda pb=pb, c=c: lambda E: E.scalar_tensor_tensor(xr[:, c, :], xr[:, c, :], ALPHA, pb[:], ALU.mult, ALU.add))(), r=[pk, ("xr", c)], w=[("xr", c)])
                    layer_norm((bufB, mean_sb, var_sb), xr, "xr", 40, 48, lambda c, tsl=tsl: xT[:, c, tsl], ("xT", tt))
                    if not last_layer:
                        DMA("pool", xres_d[:, :, tsl], xr[:], r=[("xr", c_) for c_ in range(8)], w=[("xres", tt)])
                    else:
                        ob = bufB
                        obv = bufB[:, :, :].rearrange("p a b -> p (a b)").rearrange("p (s d) -> p s d", d=D)
                        for s in range(4):
                            for cg in range(2):
                                pk, pb = mmring.next()
                                for cc in range(4):
                                    c = cg * 4 + cc
                                    A("pe", (lambda pb=pb, s=s, c=c, cc=cc: lambda E: E.transpose(pb[:, cc * 128:(cc + 1) * 128], xr[:, c, s * 128:(s + 1) * 128], idf[:]))(), r=[("xr", c), "idf"], w=[pk])
                                A("act", (lambda pb=pb, s=s, cg=cg: lambda E: E.copy(obv[:, s, cg * 512:(cg + 1) * 512], pb[:]))(), r=[pk], w=["sq"])
                        DMA("pool", out_d[sq_i, tsl, :].rearrange("(s p) d -> p s d", p=128), obv, r=["sq"], w=[("out", sq_i, tt)])

        stop = build_program.stop
        for sq_i in range(n_seq):
            if not _os.environ.get('SKIPIN'):
                phase_input(sq_i)
            SC.barrier()
            for l in range(n_layers):
                if stop < 1: break
                load_layer_params(l)
                phase_P(l)
                SC.barrier()
                if stop < 2: break
                phase_M_ret(l)
                SC.barrier()
                if stop < 3: break
                phase_M_ca(l)
                SC.barrier()
                if stop < 4: break
                phase_M_da(l)
                SC.barrier()
                if stop < 5: break
                phase_T(sq_i, l, l == n_layers - 1)
                SC.barrier()
        SC.emit(sems, block)
        build_program.stats = SC.stats
        build_program.sched = SC
    return nc


build_program.stop = 9


def _prep_inputs(inp):
    w_in = np.asarray(inp["w_in"], np.float32)
    ar = np.arange
    ret_sw = np.concatenate([h * 128 + (ar(128) + 64) % 128 for h in range(4)])
    da_sw = np.concatenate([h * 64 + (ar(64) + 32) % 64 for h in range(8)])
    cols_R = np.concatenate([ar(0, 512), ar(512, 1024), ar(4608, 5120), ar(5120, 5632)])
    cols_RS = np.concatenate([ret_sw, 512 + ret_sw, 4608 + da_sw, 5120 + da_sw])
    cols_C = ar(3072, 4096)
    cols_V = np.concatenate([ar(1024, 2048), ar(4096, 4608), ar(5632, 6144), ar(2048, 3072)])
    cols_G = np.concatenate([6144 + br * 1024 + c * 128 + ar(128) for c in range(8) for br in range(3)])
    cols = np.concatenate([cols_R, cols_RS, cols_C, cols_V, cols_G])
    assert cols.shape[0] == WI_COLS
    wi = np.ascontiguousarray(w_in[:, :, cols])
    wbr = np.ascontiguousarray(np.concatenate([inp["w_branch_a"], inp["w_branch_b"], inp["w_branch_c"]], axis=1), np.float32)
    wfi_cols = np.concatenate([np.concatenate([j * 128 + ar(128), 2816 + j * 128 + ar(128)]) for j in range(22)])
    wfi = np.ascontiguousarray(np.asarray(inp["w_ffn_in"], np.float32)[:, :, wfi_cols])
    pp = np.zeros((L, 128, 56), np.float32)
    bm = np.asarray(inp["b_merge"], np.float32)
    for c in range(8):
        for br in range(3):
            pp[:, :, c * 3 + br] = bm[:, br * 1024 + c * 128:br * 1024 + (c + 1) * 128]
    for k, nm in enumerate(("ln1_g", "ln1_b", "ln2_g", "ln2_b")):
        v = np.asarray(inp[nm], np.float32).reshape(L, 8, 128)
        pp[:, :, 24 + 8 * k:32 + 8 * k] = np.transpose(v, (0, 2, 1))
    bc = np.zeros((L, 128, 896), np.float32)
    rg = np.asarray(inp["ret_norm_g"], np.float32)
    bc[:, :, 0:256] = rg[:, None, :]
    bc[:, :, 256:512] = rg[:, None, :]
    bc[:, :, 512:640] = np.asarray(inp["da_norm_g"], np.float32)[:, None, :]
    for k, nm in enumerate(("da_lambda_q1", "da_lambda_k1", "da_lambda_q2", "da_lambda_k2")):
        bc[:, :, 640 + 64 * k:704 + 64 * k] = np.asarray(inp[nm], np.float32)[:, None, :]
    u = np.arange(1536)
    idx = np.clip(u - 511, -256, 256) + 256
    caext = np.ascontiguousarray(np.asarray(inp["ca_rel_bias"], np.float32)[:, :, idx])
    shared = dict(wi=wi, wbr=wbr, wo=np.ascontiguousarray(inp["w_out"], np.float32), wfi=wfi,
                  wfo=np.ascontiguousarray(inp["w_ffn_out"], np.float32), pp=pp, bc=bc, caext=caext,
                  rot=_rot_tables(), dec=_ret_decay(), dam=_da_mask(), cam=_ca_mask())
    return shared


def kernel(**inputs):
    x = np.ascontiguousarray(np.asarray(inputs["x"], np.float32))
    B = x.shape[0]
    n_seq = B // N_CORES
    shared = _prep_inputs(inputs)
    nc = build_program(n_seq)
    in_maps = []
    for i in range(N_CORES):
        m = dict(shared)
        m["x"] = x[i * n_seq:(i + 1) * n_seq]
        in_maps.append(m)
    res = run_bass_kernel_spmd(nc, in_maps, core_ids=list(range(N_CORES)))
    out = np.concatenate([np.asarray(r["out"]) for r in res.results], axis=0)
    return out.astype(np.float32)
```

```python
import math
from contextlib import ExitStack
import numpy as np
import concourse.bass as bass
import concourse.mybir as mybir
from concourse.bass_utils import run_bass_kernel_spmd

F32 = mybir.dt.float32
BF16 = mybir.dt.bfloat16
ALU = mybir.AluOpType
AF = mybir.ActivationFunctionType
AX = mybir.AxisListType

S = 2048
D = 1024
L = 4
NTT = 4
TT = 512
ALPHA = (2.0 * L) ** 0.25
EPS = 1e-5
NEG = -30000.0
WI_COLS = 11264
N_CORES = 8


class Sched:
    ENGS = ("pe", "act", "dve", "pool", "sp")
    DMA_RING = 8

    def __init__(self, nc):
        self.nc = nc
        self.ops = []
        self.last_w = {}
        self.readers = {}
        self.dma_count = {e: 0 for e in self.ENGS}
        self.eng_last = {e: None for e in self.ENGS}
        self.dma_last = {}
        self.pending_barrier = {e: None for e in self.ENGS}

    def add(self, eng, fn, reads=(), writes=(), dma=False):
        idx = len(self.ops)
        def _isps(k):
            return isinstance(k, str) and k.startswith("ps") and k[2:3].isdigit()
        psk = [k.split("_")[0] for k in list(reads) + list(writes) if _isps(k)]
        reads = [k for k in reads if not _isps(k)]
        writes = [k for k in writes if not _isps(k)] + sorted(set(psk))
        raw = set()
        oth = set()
        for r in reads:
            if r in self.last_w:
                raw.add(self.last_w[r])
        for w in writes:
            if w in self.last_w:
                oth.add(self.last_w[w])
            oth.update(self.readers.get(w, ()))
        if self.pending_barrier[eng] is not None:
            oth.update(self.pending_barrier[eng])
            self.pending_barrier[eng] = None
        for r in reads:
            self.readers.setdefault(r, []).append(idx)
        for w in writes:
            self.last_w[w] = idx
            self.readers[w] = []
        raw.discard(idx)
        oth.discard(idx)
        best = {}
        keep_raw, keep_oth = set(), set()
        for d in raw | oth:
            dop = self.ops[d]
            if dop["dma"]:
                (keep_raw if d in raw else keep_oth).add(d)
                continue
            if dop["eng"] == eng and not dma:
                if d not in raw:
                    continue
            cur = best.get(dop["eng"])
            if cur is None or d > cur:
                best[dop["eng"]] = d
        for d in best.values():
            (keep_raw if d in raw else keep_oth).add(d)
        raw, oth = keep_raw, keep_oth
        op = dict(eng=eng, fn=fn, raw=raw, oth=oth - raw, dma=dma, sig=False, val=None)
        if dma:
            n = self.dma_count[eng]
            self.dma_count[eng] += 1
            op["slot"] = n % self.DMA_RING
            op["val"] = 16 * (n // self.DMA_RING + 1)
            self.dma_last[(eng, op["slot"])] = idx
        self.ops.append(op)
        self.eng_last[eng] = idx
        return idx

    def barrier(self):
        pts = set(v for v in self.eng_last.values() if v is not None)
        pts.update(self.dma_last.values())
        for e in self.ENGS:
            cur = self.pending_barrier[e] or set()
            self.pending_barrier[e] = set(cur) | pts

    def emit(self, sems, block):
        ops = self.ops
        for op in ops:
            for d in op["raw"] | op["oth"]:
                dop = ops[d]
                if dop["dma"]:
                    continue
                if dop["eng"] != op["eng"]:
                    dop["sig"] = True
                elif op["dma"]:
                    dop["sig"] = True
                elif d in op["raw"] and op["eng"] in ("act", "dve", "pool"):
                    dop["sig"] = True
        cnt = {e: 0 for e in self.ENGS}
        for op in ops:
            if not op["dma"] and op["sig"]:
                cnt[op["eng"]] += 1
                op["val"] = cnt[op["eng"]]
        per_eng = {e: [] for e in self.ENGS}
        for i, op in enumerate(ops):
            per_eng[op["eng"]].append(i)
        final_dma = {}
        for op in ops:
            if op["dma"]:
                final_dma[("dma", op["eng"], op["slot"])] = op["val"]
        self.nwaits = 0

        def run(e, E):
            waited = {}
            for i in per_eng[e]:
                op = ops[i]
                need = {}
                for d in op["raw"] | op["oth"]:
                    dop = ops[d]
                    if dop["dma"]:
                        key = ("dma", dop["eng"], dop["slot"])
                    else:
                        if not dop["sig"]:
                            continue
                        if dop["eng"] == e and not (d in op["raw"] or op["dma"]):
                            continue
                        if dop["eng"] == e and e == "pe":
                            continue
                        key = dop["eng"]
                    need[key] = max(need.get(key, 0), dop["val"])
                if op["dma"] and op["val"] > 16:
                    key = ("dma", e, op["slot"])
                    need[key] = max(need.get(key, 0), op["val"] - 16)
                op["waits"] = []
                for key, v in need.items():
                    if waited.get(key, 0) >= v:
                        continue
                    E.wait_ge(sems[key], v)
                    waited[key] = v
                    self.nwaits += 1
                    op["waits"].append((key, v))
                ins = op["fn"](E)
                if op["dma"]:
                    ins.then_inc(sems[("dma", e, op["slot"])], 16)
                elif op["sig"]:
                    ins.then_inc(sems[e], 1)
            if e == "sp":
                for key, v in final_dma.items():
                    E.wait_ge(sems[key], v)

        block.tensor(lambda E: run("pe", E))
        block.scalar(lambda E: run("act", E))
        block.vector(lambda E: run("dve", E))
        block.gpsimd(lambda E: run("pool", E))
        block.sync(lambda E: run("sp", E))
        self.stats = dict(nops=len(ops), nwaits=self.nwaits, sig=cnt)
        self.per_eng = per_eng


class Ring:
    def __init__(self, items):
        self.items = items
        self.i = 0

    def next(self):
        it = self.items[self.i % len(self.items)]
        self.i += 1
        return it


def _rot_tables():
    t = np.arange(S, dtype=np.float32)
    out = np.zeros((4, 2, 128, S), np.float32)
    p = np.arange(128)
    inv = (10000.0 ** (-np.arange(0, 128, 2, dtype=np.float32) / 128)).astype(np.float32)
    ang = (t[:, None] * inv[None, :]).astype(np.float32)
    cos = np.cos(ang).astype(np.float32).T
    sin = np.sin(ang).astype(np.float32).T
    fi = p % 64
    sgn = np.where(p < 64, -1.0, 1.0).astype(np.float32)
    for ty, sc in ((0, 1.0), (1, 128 ** -0.5)):
        out[ty, 0] = cos[fi] * np.float32(sc)
        out[ty, 1] = sin[fi] * sgn[:, None] * np.float32(sc)
    inv = (10000.0 ** (-np.arange(0, 64, 2, dtype=np.float32) / 64)).astype(np.float32)
    ang = (t[:, None] * inv[None, :]).astype(np.float32)
    cos = np.cos(ang).astype(np.float32).T
    sin = np.sin(ang).astype(np.float32).T
    pp = p % 64
    fi = pp % 32
    sgn = np.where(pp < 32, -1.0, 1.0).astype(np.float32)
    for ty, sc in ((2, 64 ** -0.5), (3, 1.0)):
        out[ty, 0] = cos[fi] * np.float32(sc)
        out[ty, 1] = sin[fi] * sgn[:, None] * np.float32(sc)
    return out


def _ret_decay():
    out = np.zeros((4, 5, 128, 512), np.float64)
    i = np.arange(128)[:, None]
    j = np.arange(512)[None, :]
    for h in range(4):
        lg = math.log1p(-(2.0 ** (-5.0 - h)))
        out[h, 0] = np.exp(lg * (128 + j - i))
        for dd in range(4):
            m = dd * 128 + i
            kc = m // 64
            qc = j // 64
            rel = j - m
            out[h, 1 + dd] = np.where(kc == qc, np.exp(lg * np.abs(rel)), np.where(kc < qc, np.exp(lg * rel), 0.0))
    return out.astype(np.float32)


def _da_mask():
    out = np.zeros((4, 128, 512), np.float32)
    i = np.arange(128)[:, None]
    j = np.arange(512)[None, :]
    for dd in range(4):
        kc = (dd * 128 + i) // 64
        qc = j // 64
        out[dd] = np.where(kc <= qc, 0.0, NEG)
    return out


def _ca_mask():
    out = np.zeros((8, 128, 512), np.float32)
    i = np.arange(128)[:, None]
    j = np.arange(512)[None, :]
    for b in range(8):
        kc = 2 * b + i // 64 - 8
        a = j // 64
        out[b] = np.where((kc >= a - 8) & (kc <= a), 0.0, NEG)
    return out


def _ret_scalar(h, delta):
    lg = math.log1p(-(2.0 ** (-5.0 - h)))
    return float(math.exp(lg * (delta - 128)))


def build_program(n_seq, n_layers=L, debug=False):
    nc = bass.Bass("TRN2", target_bir_lowering=False)
    dt_in = lambda name, shape, dt=F32: nc.dram_tensor(name, shape, dt, kind="ExternalInput").ap()
    def scr(name, shape, dt):
        kind = "ExternalOutput" if (debug and name in ("qk_d", "v_d", "rg_d", "oT_d")) else "Internal"
        return nc.dram_tensor(name, shape, dt, kind=kind).ap()
    x_in = dt_in("x", [n_seq, S, D])
    wi_f = dt_in("wi", [L, D, WI_COLS])
    wbr_f = dt_in("wbr", [L, 2048, D])
    wo_f = dt_in("wo", [L, D, D])
    wfi_f = dt_in("wfi", [L, D, 5632])
    wfo_f = dt_in("wfo", [L, 2816, D])
    pp_in = dt_in("pp", [L, 128, 56])
    bc_in = dt_in("bc", [L, 128, 896])
    caext = dt_in("caext", [L, 8, 1536])
    rot_in = dt_in("rot", [4, 2, 128, S])
    dec_in = dt_in("dec", [4, 5, 128, 512])
    dam_in = dt_in("dam", [4, 128, 512])
    cam_in = dt_in("cam", [8, 128, 512])
    out_d = nc.dram_tensor("out", [n_seq, S, D], F32, kind="ExternalOutput").ap()

    wi_b = scr("wi_b", [L, D, WI_COLS], BF16)
    wbr_b = scr("wbr_b", [L, 2048, D], BF16)
    wo_b = scr("wo_b", [L, D, D], BF16)
    wfi_b = scr("wfi_b", [L, D, 5632], BF16)
    wfo_b = scr("wfo_b", [L, 2816, D], BF16)
    xres_d = scr("xres_d", [128, 8, S], F32)
    qk_d = scr("qk_d", [24, 128, S], BF16)
    v_d = scr("v_d", [S, 2048], BF16)
    rg_d = scr("rg_d", [S, 1024], F32)
    oT_d = scr("oT_d", [16, 128, S], BF16)
    with ExitStack() as es:
        SC = Sched(nc)
        sems = {}
        for e in ("pe", "act", "dve", "pool"):
            sems[e] = es.enter_context(nc.semaphore("s_" + e))
        for q in ("sp", "pool", "act"):
            for k in range(Sched.DMA_RING):
                sems[("dma", q, k)] = es.enter_context(nc.semaphore(f"d_{q}{k}"))
        psb = [es.enter_context(nc.psum_tensor(f"ps{i}", [128, 512], F32)) for i in range(7)]
        ps7 = es.enter_context(nc.psum_tensor("ps7", [128, 1024], BF16))

        uniq = [0]

        def sbuf(stack, name, shape, dt):
            uniq[0] += 1
            return stack.enter_context(nc.sbuf_tensor(f"{name}_u{uniq[0]}", shape, dt))

        xT = sbuf(es, "xT", [128, 8, S], BF16)
        idb = sbuf(es, "idb", [128, 128], BF16)
        idf = sbuf(es, "idf", [128, 128], F32)
        antiI = sbuf(es, "antiI", [128, 128], F32)
        onesf = sbuf(es, "onesf", [128, 128], F32)
        ppt = sbuf(es, "ppt", [128, 56], F32)
        bct = sbuf(es, "bct", [128, 896], F32)
        gda = sbuf(es, "gda", [128, 128], F32)
        nlam = sbuf(es, "nlam", [128, 1], F32)
        sm = sbuf(es, "sm", [128, 8], F32)
        lp = sbuf(es, "lp", [128, 64], F32)

        block = es.enter_context(nc.Block())

        def A(eng, fn, r=(), w=()):
            SC.add(eng, fn, reads=r, writes=w)

        def DMA(q, out_ap, in_ap, r=(), w=()):
            SC.add(q, lambda E: E.dma_start(out=out_ap, in_=in_ap), reads=r, writes=w, dma=True)

        def DMA2(q, out_ap, in_ap, r, wkey):
            a = out_ap.shape[1]
            hlf = a // 2
            keys = [(wkey, 0), (wkey, 1)]
            DMA(q, out_ap[:, 0:hlf, :], in_ap[:, 0:hlf, :], r=r, w=[keys[0], wkey])
            DMA(q, out_ap[:, hlf:a, :], in_ap[:, hlf:a, :], r=r, w=[keys[1], wkey])
            return keys

        A("pool", lambda E: E.memset(idb[:], 0.0), w=["idb"])
        A("pool", lambda E: E.affine_select(out=idb[:], in_=idb[:], compare_op=ALU.not_equal, fill=1.0, base=0,
                                            pattern=[[-1, 128]], channel_multiplier=1), r=["idb"], w=["idb"])
        A("pool", lambda E: E.memset(idf[:], 0.0), w=["idf"])
        A("pool", lambda E: E.affine_select(out=idf[:], in_=idf[:], compare_op=ALU.not_equal, fill=1.0, base=0,
                                            pattern=[[-1, 128]], channel_multiplier=1), r=["idf"], w=["idf"])
        A("pool", lambda E: E.memset(antiI[:], 0.0), w=["antiI"])
        A("pool", lambda E: E.affine_select(out=antiI[:], in_=antiI[:], compare_op=ALU.not_equal, fill=1.0, base=-127,
                                            pattern=[[1, 128]], channel_multiplier=1), r=["antiI"], w=["antiI"])
        A("pool", lambda E: E.memset(onesf[:], 1.0 / D), w=["onesf"])
        wkeys = {}
        import os as _os
        for l in range(n_layers if not _os.environ.get('SKIPCONV') else 0):
            for (src, dst, rows, nm) in ((wi_f, wi_b, D, "wi"), (wbr_f, wbr_b, 2048, "wbr"), (wo_f, wo_b, D, "wo"),
                                         (wfi_f, wfi_b, D, "wfi"), (wfo_f, wfo_b, 2816, "wfo")):
                ncols = src.shape[2]
                for r0 in range(0, rows, 128):
                    for c0 in range(0, ncols, 2048):
                        c1 = min(ncols, c0 + 2048)
                        DMA("pool", dst[l, r0:r0 + 128, c0:c1], src[l, r0:r0 + 128, c0:c1], w=[(nm, l, r0, c0)])
                        wkeys.setdefault((nm, l), []).append((nm, l, r0, c0))

        mmring = Ring([(f"ps{i}", psb[i]) for i in range(4)])

        def load_layer_params(l):
            DMA("sp", ppt[:], pp_in[l], w=["ppt"])
            DMA("sp", bct[:], bc_in[l], w=["bct"])
            lam_init = 0.8 - 0.6 * math.exp(-0.3 * l)
            A("dve", lambda E: E.tensor_tensor(lp[:], bct[:, 640:704], bct[:, 704:768], ALU.mult), r=["bct"], w=["lp"])
            A("dve", lambda E: E.reduce_sum(sm[:, 0:1], lp[:], axis=AX.X), r=["lp"], w=["sm0"])
            A("dve", lambda E: E.tensor_tensor(lp[:], bct[:, 768:832], bct[:, 832:896], ALU.mult), r=["bct", "sm0"], w=["lp"])
            A("dve", lambda E: E.reduce_sum(sm[:, 1:2], lp[:], axis=AX.X), r=["lp"], w=["sm1"])
            A("act", lambda E: E.activation(sm[:, 2:3], sm[:, 0:1], AF.Exp), r=["sm0"], w=["sm2"])
            A("act", lambda E: E.activation(sm[:, 3:4], sm[:, 1:2], AF.Exp), r=["sm1"], w=["sm3"])
            A("dve", lambda E: E.tensor_tensor(sm[:, 4:5], sm[:, 2:3], sm[:, 3:4], ALU.subtract), r=["sm2", "sm3"], w=["sm4"])
            A("dve", lambda E: E.tensor_scalar(nlam[:], sm[:, 4:5], lam_init, -1.0, ALU.add, ALU.mult), r=["sm4"], w=["nlam"])
            A("dve", lambda E: E.tensor_scalar_mul(gda[:], bct[:, 512:640], 1.0 - lam_init), r=["bct"], w=["gda"])

        def phase_input(sq):
            with ExitStack() as st:
                xin = [sbuf(st, f"xin{i}", [128, 4, D], F32) for i in range(2)]
                xr = [sbuf(st, f"xri{i}", [128, 8, TT], F32) for i in range(2)]
                for tt in range(NTT):
                    xi = xin[tt % 2]
                    xo = xr[tt % 2]
                    ki, ko = f"xin{tt % 2}", f"xri{tt % 2}"
                    DMA("sp", xi[:], x_in[sq, tt * TT:(tt + 1) * TT, :].rearrange("(s p) d -> p s d", p=128), w=[ki])
                    for c in range(8):
                        pk, pb = mmring.next()
                        for s in range(4):
                            A("pe", (lambda pb=pb, xi=xi, s=s, c=c: lambda E: E.transpose(pb[:, s * 128:(s + 1) * 128], xi[:, s, c * 128:(c + 1) * 128], idf[:]))(),
                              r=[ki, "idf"], w=[pk])
                        A("act", (lambda pb=pb, xo=xo, c=c: lambda E: E.copy(xo[:, c, :], pb[:]))(), r=[pk], w=[ko])
                        A("dve", (lambda xo=xo, c=c, tt=tt: lambda E: E.tensor_copy(xT[:, c, tt * TT:(tt + 1) * TT], xo[:, c, :]))(), r=[ko], w=[("xT", tt)])
                    DMA("pool", xres_d[:, :, tt * TT:(tt + 1) * TT], xo[:], r=[ko], w=[("xres", tt)])

        def phase_P(l):
            with ExitStack() as st:
                wA = [sbuf(st, f"wA{i}", [128, 8, 512], BF16) for i in range(2)]
                wB = [sbuf(st, f"wB{i}", [128, 8, 512], BF16) for i in range(2)]
                rc = sbuf(st, "rc", [128, S], F32)
                rs = sbuf(st, "rs", [128, S], F32)
                t1 = [sbuf(st, f"t1_{i}", [128, 512], F32) for i in range(2)]
                t2 = [sbuf(st, f"t2_{i}", [128, 512], F32) for i in range(2)]
                qo = [sbuf(st, f"qo{i}", [128, S], BF16) for i in range(2)]
                vb = [sbuf(st, f"vb{i}", [128, 4, 512], BF16) for i in range(2)]
                sg = [sbuf(st, f"sg{i}", [128, 512], F32) for i in range(2)]
                rgb = [sbuf(st, f"rgb{i}", [128, 4, 512], F32) for i in range(2)]
                wl = wi_b[l].rearrange("(c p) f -> p c f", p=128)
                nq = 0
                for ty in range(4):
                    wa, wb_ = wA[ty % 2], wB[ty % 2]
                    ka, kb = f"wA{ty % 2}", f"wB{ty % 2}"
                    DMA("sp", wa[:], wl[:, :, ty * 512:(ty + 1) * 512], r=wkeys[("wi", l)], w=[ka])
                    DMA("sp", wb_[:], wl[:, :, 2048 + ty * 512:2048 + (ty + 1) * 512], r=wkeys[("wi", l)], w=[kb])
                    DMA("sp", rc[:], rot_in[ty, 0], w=["rc"])
                    DMA("sp", rs[:], rot_in[ty, 1], w=["rs"])
                    for c in range(4):
                        q = qo[nq % 2]
                        kq = f"qo{nq % 2}"
                        nq += 1
                        for tt in range(NTT):
                            pka, pa = mmring.next()
                            pkb, pb = mmring.next()
                            for k in range(8):
                                A("pe", (lambda pa=pa, wa=wa, k=k, c=c, tt=tt: lambda E: E.matmul(pa[:], lhsT=wa[:, k, c * 128:(c + 1) * 128], rhs=xT[:, k, tt * TT:(tt + 1) * TT], start=(k == 0), stop=(k == 7)))(),
                                  r=[ka, ("xT", tt)], w=[pka])
                            for k in range(8):
                                A("pe", (lambda pb=pb, wb_=wb_, k=k, c=c, tt=tt: lambda E: E.matmul(pb[:], lhsT=wb_[:, k, c * 128:(c + 1) * 128], rhs=xT[:, k, tt * TT:(tt + 1) * TT], start=(k == 0), stop=(k == 7)))(),
                                  r=[kb, ("xT", tt)], w=[pkb])
                            a1, a2 = t1[tt % 2], t2[tt % 2]
                            k1, k2 = f"t1_{tt % 2}", f"t2_{tt % 2}"
                            A("dve", (lambda a1=a1, pa=pa, tt=tt: lambda E: E.tensor_tensor(a1[:], pa[:], rc[:, tt * TT:(tt + 1) * TT], ALU.mult))(), r=[pka, "rc"], w=[k1])
                            A("dve", (lambda a2=a2, pb=pb, tt=tt: lambda E: E.tensor_tensor(a2[:], pb[:], rs[:, tt * TT:(tt + 1) * TT], ALU.mult))(), r=[pkb, "rs"], w=[k2])
                            A("pool", (lambda q=q, a1=a1, a2=a2, tt=tt: lambda E: E.tensor_tensor(q[:, tt * TT:(tt + 1) * TT], a1[:], a2[:], ALU.add))(), r=[k1, k2], w=[kq])
                        ch = ty * 4 + c
                        DMA("pool", qk_d[ch], q[:], r=[kq], w=[("qk", ch)])
                wa = wA[0]
                DMA("sp", wa[:], wl[:, :, 4096:4608], r=wkeys[("wi", l)], w=["wA0"])
                wb_ = wA[1]
                DMA("sp", wb_[:], wl[:, :, 4608:5120], r=wkeys[("wi", l)], w=["wA1"])
                for qi, (wt, kw, sc) in enumerate(((wa, "wA0", 0.125), (wb_, "wA1", 1.0))):
                    for c in range(4):
                        q = qo[nq % 2]
                        kq = f"qo{nq % 2}"
                        nq += 1
                        for tt in range(NTT):
                            pka, pa = mmring.next()
                            for k in range(8):
                                A("pe", (lambda pa=pa, wt=wt, k=k, c=c, tt=tt: lambda E: E.matmul(pa[:], lhsT=wt[:, k, c * 128:(c + 1) * 128], rhs=xT[:, k, tt * TT:(tt + 1) * TT], start=(k == 0), stop=(k == 7)))(),
                                  r=[kw, ("xT", tt)], w=[pka])
                            A("act", (lambda q=q, pa=pa, tt=tt, sc=sc: lambda E: E.activation(q[:, tt * TT:(tt + 1) * TT], pa[:], AF.Identity, scale=sc))(), r=[pka], w=[kq])
                        ch = 16 + qi * 4 + c
                        DMA("pool", qk_d[ch], q[:], r=[kq], w=[("qk", ch)])
                nb = 0
                for g in range(6):
                    wt, kw = wB[g % 2], f"wB{g % 2}"
                    DMA("sp", wt[:], wl[:, :, 5120 + g * 512:5120 + (g + 1) * 512], r=wkeys[("wi", l)], w=[kw])
                    for ts in range(16):
                        pka, pa = mmring.next()
                        for k in range(8):
                            A("pe", (lambda pa=pa, wt=wt, k=k, ts=ts: lambda E: E.matmul(pa[:], lhsT=xT[:, k, ts * 128:(ts + 1) * 128], rhs=wt[:, k, :], start=(k == 0), stop=(k == 7)))(),
                              r=[kw, ("xT", ts // 4)], w=[pka])
                        if g < 4:
                            vbuf, kv = vb[nb % 2], f"vb{nb % 2}"
                            A("act", (lambda vbuf=vbuf, pa=pa, ts=ts: lambda E: E.copy(vbuf[:, ts % 4, :], pa[:]))(), r=[pka], w=[kv])
                            if ts % 4 == 3:
                                t0 = (ts // 4) * 512
                                DMA("pool", v_d[t0:t0 + 512, g * 512:(g + 1) * 512].rearrange("(s p) f -> p s f", p=128), vbuf[:], r=[kv], w=[("v", g, ts // 4)])
                                nb += 1
                        else:
                            sgt, ksg = sg[ts % 2], f"sg{ts % 2}"
                            rbuf, kr = rgb[nb % 2], f"rgb{nb % 2}"
                            A("act", (lambda sgt=sgt, pa=pa: lambda E: E.activation(sgt[:], pa[:], AF.Silu))(), r=[pka], w=[ksg])
                            A("pool", (lambda rbuf=rbuf, sgt=sgt, ts=ts: lambda E: E.tensor_tensor(rbuf[:, ts % 4, :], sgt[:], bct[:, 0:512], ALU.mult))(), r=[ksg, "bct"], w=[kr])
                            if ts % 4 == 3:
                                t0 = (ts // 4) * 512
                                DMA("pool", rg_d[t0:t0 + 512, (g - 4) * 512:(g - 3) * 512].rearrange("(s p) f -> p s f", p=128), rbuf[:], r=[kr], w=[("rg", g - 4, ts // 4)])
                                nb += 1

        string = Ring([(f"ps{i}", psb[i]) for i in range(3)])

        def rstd_from_ss(ssk, ss_ap, n, out_ap, outk):
            A("act", lambda E: E.activation(out_ap, ss_ap, AF.Ln, bias=EPS, scale=1.0 / n), r=[ssk], w=[outk])
            A("act", lambda E: E.activation(out_ap, out_ap, AF.Exp, scale=-0.5), r=[outk], w=[outk])

        LAG = 3
        NPT = 6

        def run_pipeline(items, front, back):
            n = len(items)
            for i in range(n + LAG):
                if i < n:
                    front(items[i])
                if i - LAG >= 0:
                    back(items[i - LAG])

        def phase_M_ret(l):
            with ExitStack() as st:
                qT = [sbuf(st, f"mq{i}", [128, S], BF16) for i in range(2)]
                kT = [sbuf(st, f"mk{i}", [128, S], BF16) for i in range(2)]
                vh = [sbuf(st, f"mv{i}", [128, 16, 256], BF16) for i in range(2)]
                dec = [sbuf(st, f"mdec{i}", [128, 5, 512], F32) for i in range(2)]
                rgt = [sbuf(st, f"mrg{i}", [128, 16, 256], F32) for i in range(2)]
                pt = [sbuf(st, f"mpt{i}", [128, 512], BF16) for i in range(NPT)]
                junk = sbuf(st, "mjunk", [128, 256], F32)
                ssb = [sbuf(st, f"mss{i}", [128, 2], F32) for i in range(4)]
                on = [sbuf(st, f"mon{i}", [128, 256], BF16) for i in range(2)]
                oTa = [sbuf(st, f"moT{i}", [128, 1024], BF16) for i in range(2)]
                eo = [sbuf(st, f"meo{i}", [128, 256], F32) for i in range(8)]
                accs = [(f"ps{3 + ns}", psb[3 + ns][:, 0:256]) for ns in range(4)]
                hk_ = {}

                def load_head(h):
                    b = h % 2
                    DMA("sp", qT[b][:], qk_d[h], r=[("qk", h)], w=[f"mq{b}"])
                    DMA("sp", kT[b][:], qk_d[4 + h], r=[("qk", 4 + h)], w=[f"mk{b}"])
                    kmv = DMA2("sp", vh[b][:], v_d[:, h * 256:(h + 1) * 256].rearrange("(s p) e -> p s e", p=128),
                               [("v", h // 2, i) for i in range(4)], f"mv{b}")
                    kmrg = DMA2("sp", rgt[b][:], rg_d[:, h * 256:(h + 1) * 256].rearrange("(s p) e -> p s e", p=128),
                                [("rg", h // 2, i) for i in range(4)], f"mrg{b}")
                    DMA("sp", dec[b][:], dec_in[h].rearrange("k p j -> p k j"), w=[f"mdec{b}"])
                    hk_[h] = (kmv, kmrg)

                items = []
                for h in range(4):
                    for nt in range(4):
                        for mt in range(4 * nt + 4):
                            items.append(dict(h=h, nt=nt, mt=mt, first=(nt == 0 and mt == 0), lastmt=(mt == 4 * nt + 3)))
                cnt = [0, 0]
                load_head(0)

                def front(it):
                    h, nt, mt = it["h"], it["nt"], it["mt"]
                    b = h % 2
                    delta = 512 * nt - 128 * mt
                    sk, sp_ = string.next()
                    A("pe", lambda E: E.matmul(sp_[:], lhsT=kT[b][:, mt * 128:(mt + 1) * 128], rhs=qT[b][:, nt * 512:(nt + 1) * 512], start=True, stop=True),
                      r=[f"mq{b}", f"mk{b}"], w=[sk])
                    p_, kp = pt[cnt[0] % NPT], f"mpt{cnt[0] % NPT}"
                    cnt[0] += 1
                    it["p"] = (p_, kp)
                    if delta >= 128:
                        scl = _ret_scalar(h, delta)
                        A("dve", lambda E: E.scalar_tensor_tensor(p_[:], sp_[:], scl, dec[b][:, 0, :], ALU.mult, ALU.mult), r=[sk, f"mdec{b}"], w=[kp])
                    else:
                        dd = (-delta) // 128
                        A("dve", lambda E: E.tensor_tensor(p_[:], sp_[:], dec[b][:, 1 + dd, :], ALU.mult), r=[sk, f"mdec{b}"], w=[kp])

                def back(it):
                    h, nt, mt = it["h"], it["nt"], it["mt"]
                    b = h % 2
                    p_, kp = it["p"]
                    kmv, kmrg = hk_[h]
                    if it["first"] and h + 1 < 4:
                        load_head(h + 1)
                    for ns in range(4):
                        last = 4 * nt + ns
                        if mt > last:
                            continue
                        ak, aap = accs[ns]
                        A("pe", (lambda aap=aap, ns=ns, last=last: lambda E: E.matmul(aap, lhsT=p_[:, ns * 128:(ns + 1) * 128], rhs=vh[b][:, mt, :], start=(mt == 0), stop=(mt == last)))(),
                          r=[kp, f"mv{b}"] + kmv, w=[ak])
                    if not it["lastmt"]:
                        return
                    ob, kob = oTa[cnt[1] % 2], f"moT{cnt[1] % 2}"
                    ebase = (cnt[1] % 2) * 4
                    cnt[1] += 1
                    evs = []
                    for ns in range(4):
                        ak, aap = accs[ns]
                        e_, ke = eo[ebase + ns], f"meo{ebase + ns}"
                        A("dve", (lambda e_=e_, aap=aap: lambda E: E.tensor_copy(e_[:], aap))(), r=[ak], w=[ke])
                        evs.append((ke, e_[:]))
                    for ns in range(4):
                        ak, aap = evs[ns]
                        ss, kss = ssb[ns], f"mss{ns}"
                        o_, ko = on[ns % 2], f"mon{ns % 2}"
                        A("act", (lambda aap=aap, ss=ss: lambda E: E.activation(junk[:], aap, AF.Square, accum_out=ss[:, 0:1]))(), r=[ak], w=["mjunk", kss + "a"])
                        rstd_from_ss(kss + "a", ss[:, 0:1], 256.0, ss[:, 1:2], kss + "b")
                        A("dve", (lambda o_=o_, aap=aap, ss=ss, ns=ns: lambda E: E.scalar_tensor_tensor(o_[:], aap, ss[:, 1:2], rgt[b][:, nt * 4 + ns, :], ALU.mult, ALU.mult))(),
                          r=[ak, kss + "b", f"mrg{b}"] + kmrg, w=[ko])
                        for ec in range(2):
                            A("pe", (lambda o_=o_, ec=ec, ns=ns: lambda E: E.transpose(ps7[:, ec * 512 + ns * 128:ec * 512 + (ns + 1) * 128], o_[:, ec * 128:(ec + 1) * 128], idb[:]))(),
                              r=[ko, "idb"], w=["ps7"])
                    A("act", lambda E: E.copy(ob[:], ps7[:, 0:1024]), r=["ps7"], w=[kob])
                    for ec in range(2):
                        DMA("pool", oT_d[2 * h + ec, :, nt * 512:(nt + 1) * 512], ob[:, ec * 512:(ec + 1) * 512], r=[kob], w=[("oT", 2 * h + ec, nt)])

                run_pipeline(items, front, back)

        def phase_M_ca(l):
            with ExitStack() as st:
                qT = [sbuf(st, f"cq{i}", [128, S], BF16) for i in range(2)]
                kT = [sbuf(st, f"ck{i}", [128, S], BF16) for i in range(2)]
                lv = [sbuf(st, f"clv{i}", [128, 16, 128], BF16) for i in range(4)]
                hk = [sbuf(st, f"chk{i}", [128, 8, 512], F32) for i in range(1)]
                bm32 = sbuf(st, "cbm32", [128, 8, 512], F32)
                bhi = [sbuf(st, f"cbhi{i}", [128, 8, 512], BF16) for i in range(2)]
                blo = [sbuf(st, f"cblo{i}", [128, 8, 512], BF16) for i in range(2)]
                cam = sbuf(st, "ccam", [128, 8, 512], F32)
                pt = [sbuf(st, f"cpt{i}", [128, 512], BF16) for i in range(NPT)]
                rds = [sbuf(st, f"crd{i}", [128, 512], F32) for i in range(2)]
                oc = [sbuf(st, f"coc{i}", [128, S], BF16) for i in range(2)]
                DMA("sp", cam[:], cam_in.rearrange("b p j -> p b j"), w=["ccam"])
                for i in range(4):
                    A("pool", (lambda i=i: lambda E: E.memset(lv[i][:], 1.0))(), w=[f"clv{i}"])
                hk_ = {}

                def load_head(h):
                    j, half = h // 2, h % 2
                    lo = 64 * half
                    if half == 0:
                        DMA("sp", qT[j % 2][:], qk_d[16 + j], r=[("qk", 16 + j)], w=[f"cq{j % 2}"])
                        DMA("sp", kT[j % 2][:], qk_d[20 + j], r=[("qk", 20 + j)], w=[f"ck{j % 2}"])
                    li = (j % 2) * 2 + half
                    klvs = DMA2("sp", lv[li][:, :, lo:lo + 64], v_d[:, 1024 + h * 64:1024 + (h + 1) * 64].rearrange("(s p) e -> p s e", p=128),
                                [("v", 2, i) for i in range(4)], f"clv{li}")
                    hb = h % 2
                    for b in range(8):
                        src = bass.AP(tensor=caext.tensor, offset=(l * 8 + h) * 1536 + 896 - 128 * b, ap=[[1, 128], [1, 512]])
                        DMA("sp", hk[0][:, b, :], src, w=[("chk", b)])
                    for b in range(8):
                        pk, pb = mmring.next()
                        A("pe", (lambda pb=pb, b=b: lambda E: E.matmul(pb[:], lhsT=antiI[:], rhs=hk[0][:, b, :], start=True, stop=True))(), r=[("chk", b), "antiI"], w=[pk])
                        A("dve", (lambda pb=pb, b=b: lambda E: E.tensor_tensor(bm32[:, b, :], pb[:], cam[:, b, :], ALU.add))(), r=[pk, "ccam"], w=[("cbm32", b)])
                        A("act", (lambda b=b: lambda E: E.copy(bhi[hb][:, b, :], bm32[:, b, :]))(), r=[("cbm32", b)], w=[("cbhi", hb, b)])
                        A("dve", (lambda b=b: lambda E: E.tensor_tensor(blo[hb][:, b, :], bm32[:, b, :], bhi[hb][:, b, :], ALU.subtract))(), r=[("cbm32", b), ("cbhi", hb, b)], w=[("cblo", hb, b)])
                    hk_[h] = klvs

                items = []
                for h in range(8):
                    for nt in range(4):
                        bs = [b for b in range(8) if 4 * nt - 4 + b >= 0]
                        for b in bs:
                            items.append(dict(h=h, nt=nt, b=b, first=(nt == 0 and b == bs[0]), b0=bs[0], lastb=(b == bs[-1])))
                cnt = [0, 0]
                load_head(0)

                def front(it):
                    h, nt, b = it["h"], it["nt"], it["b"]
                    j, half = h // 2, h % 2
                    lo = 64 * half
                    mt = 4 * nt - 4 + b
                    sk, sp_ = string.next()
                    A("pe", lambda E: E.matmul(sp_[:], lhsT=kT[j % 2][lo:lo + 64, mt * 128:(mt + 1) * 128], rhs=qT[j % 2][lo:lo + 64, nt * 512:(nt + 1) * 512], start=True, stop=False),
                      r=[f"cq{j % 2}", f"ck{j % 2}"], w=[sk])
                    A("pe", lambda E: E.matmul(sp_[:], lhsT=idb[:], rhs=bhi[h % 2][:, b, :], start=False, stop=False), r=["idb", ("cbhi", h % 2, b)], w=[sk])
                    A("pe", lambda E: E.matmul(sp_[:], lhsT=idb[:], rhs=blo[h % 2][:, b, :], start=False, stop=True), r=["idb", ("cblo", h % 2, b)], w=[sk])
                    p_, kp = pt[cnt[0] % NPT], f"cpt{cnt[0] % NPT}"
                    cnt[0] += 1
                    it["p"] = (p_, kp)
                    A("act", lambda E: E.activation(p_[:], sp_[:], AF.Exp), r=[sk], w=[kp])

                def back(it):
                    h, nt, b = it["h"], it["nt"], it["b"]
                    j, half = h // 2, h % 2
                    lo = 64 * half
                    li = (j % 2) * 2 + half
                    mt = 4 * nt - 4 + b
                    p_, kp = it["p"]
                    klvs = hk_[h]
                    if it["first"] and h + 1 < 8:
                        load_head(h + 1)
                    bi = 4 + (h * 4 + nt) % 2
                    ok_, ob = f"ps{bi}", psb[bi]
                    A("pe", lambda E: E.matmul(ob[:], lhsT=lv[li][:, mt, :], rhs=p_[:], start=(b == it["b0"]), stop=it["lastb"]),
                      r=[kp, f"clv{li}"] + klvs, w=[ok_])
                    if not it["lastb"]:
                        return
                    ocb, kocb = oc[j % 2], f"coc{j % 2}"
                    dl = 64 - lo
                    rd, krd = rds[bi - 4], f"crd{bi - 4}"
                    A("dve", lambda E: E.tensor_copy(rd[lo:lo + 64, :], ob[dl:dl + 64, :]), r=[ok_], w=[krd])
                    A("dve", lambda E: E.reciprocal(rd[lo:lo + 64, :], rd[lo:lo + 64, :]), r=[krd], w=[krd])
                    A("dve", lambda E: E.tensor_tensor(ocb[lo:lo + 64, nt * 512:(nt + 1) * 512], ob[lo:lo + 64, :], rd[lo:lo + 64, :], ALU.mult),
                      r=[ok_, krd], w=[kocb])
                    if half == 1 and nt == 3:
                        DMA("pool", oT_d[8 + j], ocb[:], r=[kocb], w=[("oT", 8 + j, i) for i in range(4)])

                run_pipeline(items, front, back)

        def phase_M_da(l):
            with ExitStack() as st:
                qT = [sbuf(st, f"dq{i}", [128, S], BF16) for i in range(2)]
                kT = [sbuf(st, f"dk{i}", [128, S], BF16) for i in range(2)]
                va = [sbuf(st, f"dva{i}", [128, 16, 129], BF16) for i in range(2)]
                dam = sbuf(st, "ddam", [128, 4, 512], F32)
                tmp = [sbuf(st, f"dtmp{i}", [128, 512], F32) for i in range(2)]
                pt = [sbuf(st, f"dpt{i}", [128, 512], BF16) for i in range(NPT)]
                r_ = [sbuf(st, f"dr{i}", [128, 4], F32) for i in range(4)]
                u0 = [sbuf(st, f"du{i}", [128, 128], F32) for i in range(4)]
                dd_ = [sbuf(st, f"dd{i}", [128, 128], F32) for i in range(2)]
                junk = sbuf(st, "djunk", [128, 128], F32)
                on = [sbuf(st, f"don{i}", [128, 128], BF16) for i in range(2)]
                oc = [sbuf(st, f"doc{i}", [128, 512], BF16) for i in range(2)]
                ea = [sbuf(st, f"dea{i}", [128, 129], F32) for i in range(8)]
                accs = [(f"ps{3 + ns}", psb[3 + ns][:, 0:129]) for ns in range(4)]
                DMA("sp", dam[:], dam_in.rearrange("k p j -> p k j"), w=["ddam"])
                for i in range(2):
                    A("pool", (lambda i=i: lambda E: E.memset(va[i][:], 1.0))(), w=[f"dva{i}"])
                hk_ = {}

                def load_head(h):
                    b = h % 2
                    DMA("sp", qT[b][:], qk_d[8 + h], r=[("qk", 8 + h)], w=[f"dq{b}"])
                    DMA("sp", kT[b][:], qk_d[12 + h], r=[("qk", 12 + h)], w=[f"dk{b}"])
                    hk_[h] = DMA2("sp", va[b][:, :, 0:128], v_d[:, 1536 + h * 128:1536 + (h + 1) * 128].rearrange("(s p) e -> p s e", p=128),
                                  [("v", 3, i) for i in range(4)], f"dva{b}")

                items = []
                for h in range(4):
                    for nt in range(4):
                        for g in range(2):
                            for mt in range(4 * nt + 4):
                                items.append(dict(h=h, nt=nt, g=g, mt=mt, first=(nt == 0 and g == 0 and mt == 0), lastmt=(mt == 4 * nt + 3)))
                cnt = [0, 0, 0]
                load_head(0)

                def front(it):
                    h, nt, g, mt = it["h"], it["nt"], it["g"], it["mt"]
                    b = h % 2
                    lo = 64 * g
                    delta = 512 * nt - 128 * mt
                    sk, sp_ = string.next()
                    A("pe", lambda E: E.matmul(sp_[:], lhsT=kT[b][lo:lo + 64, mt * 128:(mt + 1) * 128], rhs=qT[b][lo:lo + 64, nt * 512:(nt + 1) * 512], start=True, stop=True),
                      r=[f"dq{b}", f"dk{b}"], w=[sk])
                    p_, kp = pt[cnt[0] % NPT], f"dpt{cnt[0] % NPT}"
                    cnt[0] += 1
                    it["p"] = (p_, kp)
                    if delta >= 128:
                        A("act", lambda E: E.activation(p_[:], sp_[:], AF.Exp), r=[sk], w=[kp])
                    else:
                        dd = (-delta) // 128
                        t_, kt = tmp[cnt[1] % 2], f"dtmp{cnt[1] % 2}"
                        cnt[1] += 1
                        A("dve", lambda E: E.tensor_tensor(t_[:], sp_[:], dam[:, dd, :], ALU.add), r=[sk, "ddam"], w=[kt])
                        A("act", lambda E: E.activation(p_[:], t_[:], AF.Exp), r=[kt], w=[kp])

                def back(it):
                    h, nt, g, mt = it["h"], it["nt"], it["g"], it["mt"]
                    b = h % 2
                    p_, kp = it["p"]
                    kdva = hk_[h]
                    if it["first"] and h + 1 < 4:
                        load_head(h + 1)
                    for ns in range(4):
                        last = 4 * nt + ns
                        if mt > last:
                            continue
                        ak, aap = accs[ns]
                        A("pe", (lambda aap=aap, ns=ns, last=last: lambda E: E.matmul(aap, lhsT=p_[:, ns * 128:(ns + 1) * 128], rhs=va[b][:, mt, :], start=(mt == 0), stop=(mt == last)))(),
                          r=[kp, f"dva{b}"] + kdva, w=[ak])
                    if not it["lastmt"]:
                        return
                    ebase = g * 4
                    evs = []
                    for ns in range(4):
                        ak, aap = accs[ns]
                        e_, ke = ea[ebase + ns], f"dea{ebase + ns}"
                        A("dve", (lambda e_=e_, aap=aap: lambda E: E.tensor_copy(e_[:], aap))(), r=[ak], w=[ke])
                        evs.append((ke, e_[:]))
                    for ns in range(4):
                        ak, aap = evs[ns]
                        rr, kr = r_[ns], f"dr{ns}"
                        u_, ku = u0[ns], f"du{ns}"
                        if g == 0:
                            A("dve", (lambda rr=rr, aap=aap: lambda E: E.reciprocal(rr[:, 0:1], aap[:, 128:129]))(), r=[ak], w=[kr + "0"])
                            A("dve", (lambda u_=u_, aap=aap, rr=rr: lambda E: E.tensor_scalar_mul(u_[:], aap[:, 0:128], rr[:, 0:1]))(), r=[ak, kr + "0"], w=[ku])
                        else:
                            d_, kd = dd_[ns % 2], f"dd{ns % 2}"
                            o_, ko = on[ns % 2], f"don{ns % 2}"
                            A("dve", (lambda rr=rr, aap=aap: lambda E: E.reciprocal(rr[:, 1:2], aap[:, 128:129]))(), r=[ak], w=[kr + "1"])
                            A("dve", (lambda rr=rr: lambda E: E.tensor_tensor(rr[:, 1:2], rr[:, 1:2], nlam[:], ALU.mult))(), r=[kr + "1", "nlam"], w=[kr + "1"])
                            A("dve", (lambda d_=d_, aap=aap, rr=rr, u_=u_: lambda E: E.scalar_tensor_tensor(d_[:], aap[:, 0:128], rr[:, 1:2], u_[:], ALU.mult, ALU.add))(), r=[ak, kr + "1", ku], w=[kd])
                            A("act", (lambda d_=d_, rr=rr: lambda E: E.activation(junk[:], d_[:], AF.Square, accum_out=rr[:, 2:3]))(), r=[kd], w=["djunk", kr + "2"])
                            rstd_from_ss(kr + "2", rr[:, 2:3], 128.0, rr[:, 3:4], kr + "3")
                            A("dve", (lambda o_=o_, d_=d_, rr=rr: lambda E: E.scalar_tensor_tensor(o_[:], d_[:], rr[:, 3:4], gda[:], ALU.mult, ALU.mult))(), r=[kd, kr + "3", "gda"], w=[ko])
                            A("pe", (lambda o_=o_, ns=ns: lambda E: E.transpose(ps7[:, ns * 128:(ns + 1) * 128], o_[:], idb[:]))(), r=[ko, "idb"], w=["ps7"])
                    if g == 1:
                        ob, kob = oc[cnt[2] % 2], f"doc{cnt[2] % 2}"
                        cnt[2] += 1
                        A("act", lambda E: E.copy(ob[:], ps7[:, 0:512]), r=["ps7"], w=[kob])
                        DMA("pool", oT_d[12 + h, :, nt * 512:(nt + 1) * 512], ob[:], r=[kob], w=[("oT", 12 + h, nt)])

                run_pipeline(items, front, back)

        def layer_norm(st_bufs, xr, kxr, gcol, bcol, bf_out, bf_key):
            sq, mean_sb, var_sb = st_bufs
            for c in range(8):
                A("pe", (lambda c=c: lambda E: E.matmul(psb[4][:], lhsT=onesf[:], rhs=xr[:, c, :], start=(c == 0), stop=(c == 7)))(), r=[(kxr, c), "onesf"], w=["ps4"])
            for c in range(8):
                A("act", (lambda c=c: lambda E: E.activation(sq[:, c, :], xr[:, c, :], AF.Square))(), r=[(kxr, c)], w=[("sq", c), "sq"])
            for c in range(8):
                A("pe", (lambda c=c: lambda E: E.matmul(psb[5][:], lhsT=onesf[:], rhs=sq[:, c, :], start=(c == 0), stop=(c == 7)))(), r=[("sq", c), "onesf"], w=["ps5"])
            A("act", lambda E: E.copy(mean_sb[:], psb[4][:]), r=["ps4"], w=["mean"])
            A("dve", lambda E: E.tensor_tensor(var_sb[:], mean_sb[:], mean_sb[:], ALU.mult), r=["mean"], w=["var"])
            A("dve", lambda E: E.tensor_tensor(var_sb[:], psb[5][:], var_sb[:], ALU.subtract), r=["ps5", "var"], w=["var"])
            A("act", lambda E: E.activation(var_sb[:], var_sb[:], AF.Ln, bias=EPS, scale=1.0), r=["var"], w=["var"])
            A("act", lambda E: E.activation(var_sb[:], var_sb[:], AF.Exp, scale=-0.5), r=["var"], w=["var"])
            for c in range(8):
                eng = "pool" if c in (2, 6) else "dve"
                A(eng, (lambda c=c: lambda E: E.tensor_tensor(xr[:, c, :], xr[:, c, :], mean_sb[:], ALU.subtract))(), r=[(kxr, c), "mean"], w=[(kxr, c)])
                A(eng, (lambda c=c: lambda E: E.tensor_tensor(xr[:, c, :], xr[:, c, :], var_sb[:], ALU.mult))(), r=[(kxr, c), "var"], w=[(kxr, c)])
                A("act", (lambda c=c: lambda E: E.activation(xr[:, c, :], xr[:, c, :], AF.Identity, bias=ppt[:, bcol + c:bcol + c + 1], scale=ppt[:, gcol + c:gcol + c + 1]))(), r=[(kxr, c), "ppt"], w=[(kxr, c)])
                A("dve", (lambda c=c: lambda E: E.tensor_copy(bf_out(c), xr[:, c, :]))(), r=[(kxr, c)], w=[bf_key])

        def phase_T(sq_i, l, last_layer):
            with ExitStack() as st:
                xr = sbuf(st, "xr", [128, 8, TT], F32)
                bufA = sbuf(st, "bufA", [128, 22 * TT], BF16)
                bufB = sbuf(st, "bufB", [128, 8, TT], F32)
                wg = [sbuf(st, f"wg{i}", [128, 8, 384], BF16) for i in range(2)]
                wbr = [sbuf(st, f"wbr{i}", [128, 16, 128], BF16) for i in range(2)]
                gsb = [sbuf(st, f"gsb{i}", [128, TT], F32) for i in range(3)]
                mb_ = [sbuf(st, f"mb{i}", [128, TT], F32) for i in range(3)]
                merged = sbuf(st, "merged", [128, 8, TT], BF16)
                wo_ = [sbuf(st, f"wo{i}", [128, 8, 128], BF16) for i in range(2)]
                mean_sb = sbuf(st, "mean_sb", [128, TT], F32)
                var_sb = sbuf(st, "var_sb", [128, TT], F32)
                x1b = sbuf(st, "x1b", [128, 8, TT], BF16)
                wfi = [sbuf(st, f"wfi{i}", [128, 8, 256], BF16) for i in range(3)]
                sgl = [sbuf(st, f"sgl{i}", [128, TT], F32) for i in range(2)]
                wfo = [sbuf(st, f"wfo{i}", [128, 22, 128], BF16) for i in range(2)]
                oTv = bufA[:, 0:16 * TT].rearrange("p (e t) -> p e t", t=TT)
                aTv = bufA[:, :].rearrange("p (e t) -> p e t", t=TT)
                wil = wi_b[l].rearrange("(c p) f -> p c f", p=128)
                wbl = wbr_b[l].rearrange("(e p) f -> p e f", p=128)
                wol = wo_b[l].rearrange("(c p) f -> p c f", p=128)
                wfil = wfi_b[l].rearrange("(c p) f -> p c f", p=128)
                wfol = wfo_b[l].rearrange("(j p) f -> p j f", p=128)
                for tt in range(NTT):
                    tsl = slice(tt * TT, (tt + 1) * TT)
                    DMA("sp", xr[:], xres_d[:, :, tsl], r=[("xres", tt)], w=[("xr", c_) for c_ in range(8)])
                    kbA = DMA2("sp", oTv, oT_d[:, :, tsl].rearrange("e p t -> p e t"), [("oT", e, tt) for e in range(16)], "bufA")
                    for c in range(8):
                        wgt, kwg = wg[c % 2], f"wg{c % 2}"
                        wbt, kwb = wbr[c % 2], f"wbr{c % 2}"
                        DMA("sp", wgt[:], wil[:, :, 8192 + c * 384:8192 + (c + 1) * 384], r=wkeys[("wi", l)], w=[kwg])
                        DMA("sp", wbt[:], wbl[:, :, c * 128:(c + 1) * 128], r=wkeys[("wbr", l)], w=[kwb])
                        for br in range(3):
                            pk, pb = mmring.next()
                            for k in range(8):
                                A("pe", (lambda pb=pb, wgt=wgt, k=k, br=br, tsl=tsl: lambda E: E.matmul(pb[:], lhsT=wgt[:, k, br * 128:(br + 1) * 128], rhs=xT[:, k, tsl], start=(k == 0), stop=(k == 7)))(),
                                  r=[kwg, ("xT", tt)], w=[pk])
                            A("act", (lambda pb=pb, br=br, c=c: lambda E: E.activation(gsb[br][:], pb[:], AF.Sigmoid, bias=ppt[:, c * 3 + br:c * 3 + br + 1]))(), r=[pk, "ppt"], w=[f"gsb{br}"])
                        for br, (e0, e1) in enumerate(((0, 8), (8, 12), (12, 16))):
                            pk, pb = mmring.next()
                            for e in range(e0, e1):
                                A("pe", (lambda pb=pb, wbt=wbt, e=e, e0=e0, e1=e1: lambda E: E.matmul(pb[:], lhsT=wbt[:, e, :], rhs=oTv[:, e, :], start=(e == e0), stop=(e == e1 - 1)))(),
                                  r=[kwb, "bufA"] + kbA, w=[pk])
                            A("dve", (lambda pb=pb, br=br: lambda E: E.tensor_tensor(mb_[br][:], pb[:], gsb[br][:], ALU.mult))(), r=[pk, f"gsb{br}"], w=[f"mb{br}"])
                        A("pool", lambda E: E.tensor_tensor(mb_[0][:], mb_[0][:], mb_[1][:], ALU.add), r=["mb0", "mb1"], w=["mb0"])
                        A("pool", (lambda c=c: lambda E: E.tensor_tensor(merged[:, c, :], mb_[0][:], mb_[2][:], ALU.add))(), r=["mb0", "mb2"], w=["merged"])
                    for c in range(8):
                        wt, kw = wo_[c % 2], f"wo{c % 2}"
                        DMA("sp", wt[:], wol[:, :, c * 128:(c + 1) * 128], r=wkeys[("wo", l)], w=[kw])
                        pk, pb = mmring.next()
                        for k in range(8):
                            A("pe", (lambda pb=pb, wt=wt, k=k: lambda E: E.matmul(pb[:], lhsT=wt[:, k, :], rhs=merged[:, k, :], start=(k == 0), stop=(k == 7)))(), r=[kw, "merged"], w=[pk])
                        A("dve", (lambda pb=pb, c=c: lambda E: E.scalar_tensor_tensor(xr[:, c, :], xr[:, c, :], ALPHA, pb[:], ALU.mult, ALU.add))(), r=[pk, ("xr", c)], w=[("xr", c)])
                    layer_norm((bufB, mean_sb, var_sb), xr, "xr", 24, 32, lambda c: x1b[:, c, :], "x1b")
                    for j in range(22):
                        wt, kw = wfi[j % 3], f"wfi{j % 3}"
                        DMA("sp", wt[:], wfil[:, :, j * 256:(j + 1) * 256], r=wkeys[("wfi", l)], w=[kw])
                        pkg, pg = mmring.next()
                        pku, pu = mmring.next()
                        for k in range(8):
                            A("pe", (lambda pg=pg, wt=wt, k=k: lambda E: E.matmul(pg[:], lhsT=wt[:, k, 0:128], rhs=x1b[:, k, :], start=(k == 0), stop=(k == 7)))(), r=[kw, "x1b"], w=[pkg])
                        for k in range(8):
                            A("pe", (lambda pu=pu, wt=wt, k=k: lambda E: E.matmul(pu[:], lhsT=wt[:, k, 128:256], rhs=x1b[:, k, :], start=(k == 0), stop=(k == 7)))(), r=[kw, "x1b"], w=[pku])
                        sgt, ksg = sgl[j % 2], f"sgl{j % 2}"
                        A("act", (lambda sgt=sgt, pg=pg: lambda E: E.activation(sgt[:], pg[:], AF.Silu))(), r=[pkg], w=[ksg])
                        A("dve", (lambda sgt=sgt, pu=pu, j=j: lambda E: E.tensor_tensor(aTv[:, j, :], pu[:], sgt[:], ALU.mult))(), r=[pku, ksg], w=["bufA"])
                    for c in range(8):
                        wt, kw = wfo[c % 2], f"wfo{c % 2}"
                        DMA("sp", wt[:], wfol[:, :, c * 128:(c + 1) * 128], r=wkeys[("wfo", l)], w=[kw])
                        pk, pb = mmring.next()
                        for j in range(22):
                            A("pe", (lambda pb=pb, wt=wt, j=j: lambda E: E.matmul(pb[:], lhsT=wt[:, j, :], rhs=aTv[:, j, :], start=(j == 0), stop=(j == 21)))(), r=[kw, "bufA"], w=[pk])
                        A("dve", (lambda pb=pb, c=c: lambda E: E.scalar_tensor_tensor(xr[:, c, :], xr[:, c, :], ALPHA, pb[:], ALU.mult, ALU.add))(), r=[pk, ("xr", c)], w=[("xr", c)])
                    layer_norm((bufB, mean_sb, var_sb), xr, "xr", 40, 48, lambda c, tsl=tsl: xT[:, c, tsl], ("xT", tt))
                    if not last_layer:
                        DMA("pool", xres_d[:, :, tsl], xr[:], r=[("xr", c_) for c_ in range(8)], w=[("xres", tt)])
                    else:
                        ob = bufB
                        obv = bufB[:, :, :].rearrange("p a b -> p (a b)").rearrange("p (s d) -> p s d", d=D)
                        for s in range(4):
                            for cg in range(2):
                                pk, pb = mmring.next()
                                for cc in range(4):
                                    c = cg * 4 + cc
                                    A("pe", (lambda pb=pb, s=s, c=c, cc=cc: lambda E: E.transpose(pb[:, cc * 128:(cc + 1) * 128], xr[:, c, s * 128:(s + 1) * 128], idf[:]))(), r=[("xr", c), "idf"], w=[pk])
                                A("act", (lambda pb=pb, s=s, cg=cg: lambda E: E.copy(obv[:, s, cg * 512:(cg + 1) * 512], pb[:]))(), r=[pk], w=["sq"])
                        DMA("pool", out_d[sq_i, tsl, :].rearrange("(s p) d -> p s d", p=128), obv, r=["sq"], w=[("out", sq_i, tt)])

        stop = build_program.stop
        for sq_i in range(n_seq):
            if not _os.environ.get('SKIPIN'):
                phase_input(sq_i)
            SC.barrier()
            for l in range(n_layers):
                if stop < 1: break
                load_layer_params(l)
                phase_P(l)
                SC.barrier()
                if stop < 2: break
                phase_M_ret(l)
                SC.barrier()
                if stop < 3: break
                phase_M_ca(l)
                SC.barrier()
                if stop < 4: break
                phase_M_da(l)
                SC.barrier()
                if stop < 5: break
                phase_T(sq_i, l, l == n_layers - 1)
                SC.barrier()
        SC.emit(sems, block)
        build_program.stats = SC.stats
        build_program.sched = SC
    return nc


build_program.stop = 9


def _prep_inputs(inp):
    w_in = np.asarray(inp["w_in"], np.float32)
    ar = np.arange
    ret_sw = np.concatenate([h * 128 + (ar(128) + 64) % 128 for h in range(4)])
    da_sw = np.concatenate([h * 64 + (ar(64) + 32) % 64 for h in range(8)])
    cols_R = np.concatenate([ar(0, 512), ar(512, 1024), ar(4608, 5120), ar(5120, 5632)])
    cols_RS = np.concatenate([ret_sw, 512 + ret_sw, 4608 + da_sw, 5120 + da_sw])
    cols_C = ar(3072, 4096)
    cols_V = np.concatenate([ar(1024, 2048), ar(4096, 4608), ar(5632, 6144), ar(2048, 3072)])
    cols_G = np.concatenate([6144 + br * 1024 + c * 128 + ar(128) for c in range(8) for br in range(3)])
    cols = np.concatenate([cols_R, cols_RS, cols_C, cols_V, cols_G])
    assert cols.shape[0] == WI_COLS
    wi = np.ascontiguousarray(w_in[:, :, cols])
    wbr = np.ascontiguousarray(np.concatenate([inp["w_branch_a"], inp["w_branch_b"], inp["w_branch_c"]], axis=1), np.float32)
    wfi_cols = np.concatenate([np.concatenate([j * 128 + ar(128), 2816 + j * 128 + ar(128)]) for j in range(22)])
    wfi = np.ascontiguousarray(np.asarray(inp["w_ffn_in"], np.float32)[:, :, wfi_cols])
    pp = np.zeros((L, 128, 56), np.float32)
    bm = np.asarray(inp["b_merge"], np.float32)
    for c in range(8):
        for br in range(3):
            pp[:, :, c * 3 + br] = bm[:, br * 1024 + c * 128:br * 1024 + (c + 1) * 128]
    for k, nm in enumerate(("ln1_g", "ln1_b", "ln2_g", "ln2_b")):
        v = np.asarray(inp[nm], np.float32).reshape(L, 8, 128)
        pp[:, :, 24 + 8 * k:32 + 8 * k] = np.transpose(v, (0, 2, 1))
    bc = np.zeros((L, 128, 896), np.float32)
    rg = np.asarray(inp["ret_norm_g"], np.float32)
    bc[:, :, 0:256] = rg[:, None, :]
    bc[:, :, 256:512] = rg[:, None, :]
    bc[:, :, 512:640] = np.asarray(inp["da_norm_g"], np.float32)[:, None, :]
    for k, nm in enumerate(("da_lambda_q1", "da_lambda_k1", "da_lambda_q2", "da_lambda_k2")):
        bc[:, :, 640 + 64 * k:704 + 64 * k] = np.asarray(inp[nm], np.float32)[:, None, :]
    u = np.arange(1536)
    idx = np.clip(u - 511, -256, 256) + 256
    caext = np.ascontiguousarray(np.asarray(inp["ca_rel_bias"], np.float32)[:, :, idx])
    shared = dict(wi=wi, wbr=wbr, wo=np.ascontiguousarray(inp["w_out"], np.float32), wfi=wfi,
                  wfo=np.ascontiguousarray(inp["w_ffn_out"], np.float32), pp=pp, bc=bc, caext=caext,
                  rot=_rot_tables(), dec=_ret_decay(), dam=_da_mask(), cam=_ca_mask())
    return shared


def kernel(**inputs):
    x = np.ascontiguousarray(np.asarray(inputs["x"], np.float32))
    B = x.shape[0]
    n_seq = B // N_CORES
    shared = _prep_inputs(inputs)
    nc = build_program(n_seq)
    in_maps = []
    for i in range(N_CORES):
        m = dict(shared)
        m["x"] = x[i * n_seq:(i + 1) * n_seq]
        in_maps.append(m)
    res = run_bass_kernel_spmd(nc, in_maps, core_ids=list(range(N_CORES)))
    out = np.concatenate([np.asarray(r["out"]) for r in res.results], axis=0)
    return out.astype(np.float32)
```
